# Optimizing a Trainium2 kernel written in Bass

```python
import math
import jax, jax.numpy as jnp
from jax import lax
import numpy as np

D_MODEL = 1024
BATCH = 8
SEQ = 4096
DEPTH = 4

N_A_LAYERS = DEPTH // 2
N_B_LAYERS = DEPTH - N_A_LAYERS
CONV_WIDTH = 31
HEAD_DIM = 64
N_HEADS = D_MODEL // HEAD_DIM
N_KV_HEADS = N_HEADS // 4
GROUP = N_HEADS // N_KV_HEADS
WINDOW = 128
BLOCK = 128
N_BUCKETS = 32
MAX_DISTANCE = 128
D_FF = -(-8 * D_MODEL // (3 * 256)) * 256
EPS = 1e-6
NEG_INF = -1e30

kernel_name = "yoco_conformer_swa_sink_hybrid"


def rmsnorm(x, g):
    xf = x.astype(jnp.float32)
    xf = xf * lax.rsqrt(jnp.mean(xf * xf, axis=-1, keepdims=True) + EPS)
    return (xf * g.astype(jnp.float32)).astype(x.dtype)


def layernorm(x, g, b):
    xf = x.astype(jnp.float32)
    mu = jnp.mean(xf, axis=-1, keepdims=True)
    var = jnp.mean(jnp.square(xf - mu), axis=-1, keepdims=True)
    y = (xf - mu) * lax.rsqrt(var + EPS) * g.astype(jnp.float32) + b.astype(jnp.float32)
    return y.astype(x.dtype)


def swiglu_ffn(x, w_up, w_down):
    gate, up = jnp.split(x @ w_up, 2, axis=-1)
    return (jax.nn.silu(gate) * up) @ w_down


def conformer_conv(x, w_pw1, b_pw1, w_dw, b_dw, ln_g, ln_b, w_pw2, b_pw2):
    a = jax.nn.glu(x @ w_pw1 + b_pw1, axis=-1)
    y = lax.conv_general_dilated(
        a, w_dw[:, None, :].astype(a.dtype), window_strides=(1,),
        padding=((CONV_WIDTH - 1, 0),),
        dimension_numbers=('NWC', 'WIO', 'NWC'),
        feature_group_count=D_MODEL) + b_dw
    y = jax.nn.silu(layernorm(y, ln_g, ln_b))
    return y @ w_pw2 + b_pw2


def t5_causal_bucket(dist):
    max_exact = N_BUCKETS // 2
    d = jnp.maximum(dist, 0)
    log_ratio = jnp.log(jnp.maximum(d, 1).astype(jnp.float32) / max_exact) / math.log(MAX_DISTANCE / max_exact)
    large = max_exact + (log_ratio * (N_BUCKETS - max_exact)).astype(jnp.int32)
    large = jnp.minimum(large, N_BUCKETS - 1)
    return jnp.where(d < max_exact, d, large)


def banded_sink_attention(q, k, v, sinks, rel_bias):
    B, S = q.shape[0], q.shape[1]
    nb = S // BLOCK
    qb = q.reshape(B, nb, BLOCK, N_KV_HEADS, GROUP, HEAD_DIM)
    kb = k.reshape(B, nb, BLOCK, N_KV_HEADS, HEAD_DIM)
    vb = v.reshape(B, nb, BLOCK, N_KV_HEADS, HEAD_DIM)
    pad = ((0, 0), (1, 0), (0, 0), (0, 0), (0, 0))
    k_band = jnp.concatenate([jnp.pad(kb, pad)[:, :-1], kb], axis=2)
    v_band = jnp.concatenate([jnp.pad(vb, pad)[:, :-1], vb], axis=2)

    s = jnp.einsum('bnqhgd,bnkhd->bnhgqk', qb, k_band,
                   preferred_element_type=jnp.float32) * (HEAD_DIM ** -0.5)

    qi = jnp.arange(BLOCK, dtype=jnp.int32)
    kj = jnp.arange(2 * BLOCK, dtype=jnp.int32)
    dist = qi[:, None] + BLOCK - kj[None, :]
    in_window = (dist >= 0) & (dist < WINDOW)
    bias = rel_bias.astype(jnp.float32)[t5_causal_bucket(dist)]
    bias = jnp.transpose(bias, (2, 0, 1)).reshape(N_KV_HEADS, GROUP, BLOCK, 2 * BLOCK)
    key_pos = (jnp.arange(nb, dtype=jnp.int32)[:, None] - 1) * BLOCK + kj[None, :]
    mask = in_window[None, :, :] & (key_pos >= 0)[:, None, :]

    s = jnp.where(mask[None, :, None, None], s + bias, NEG_INF)
    sink = sinks.astype(jnp.float32).reshape(N_KV_HEADS, GROUP, 1, 1)
    m = jnp.maximum(jnp.max(s, axis=-1, keepdims=True), sink)
    p = jnp.exp(s - m)
    probs = p / (jnp.sum(p, axis=-1, keepdims=True) + jnp.exp(sink - m))
    o = jnp.einsum('bnhgqk,bnkhd->bnqhgd', probs.astype(v.dtype), v_band)
    return o.reshape(B, S, N_HEADS * HEAD_DIM)


def setup_inputs(seed: int = 0) -> dict:
    key = jax.random.key(seed)
    ks = jax.random.split(key, 24)
    f32 = jnp.float32
    D, HD, KVD = D_MODEL, N_HEADS * HEAD_DIM, N_KV_HEADS * HEAD_DIM
    nrm = lambda k, shape, scale: jax.random.normal(k, shape, f32) * scale
    gain = lambda k, shape: 1.0 + 0.05 * jax.random.normal(k, shape, f32)
    return {
        "x": jax.random.normal(ks[0], (BATCH, SEQ, D), f32),
        "norm_mix": gain(ks[1], (DEPTH, D)),
        "norm_ffn": gain(ks[2], (DEPTH, D)),
        "conv_w_pw1": nrm(ks[3], (N_A_LAYERS, D, 2 * D), D ** -0.5),
        "conv_b_pw1": nrm(ks[4], (N_A_LAYERS, 2 * D), 0.02),
        "conv_w_dw": nrm(ks[5], (N_A_LAYERS, CONV_WIDTH, D), CONV_WIDTH ** -0.5),
        "conv_b_dw": nrm(ks[6], (N_A_LAYERS, D), 0.02),
        "conv_ln_g": gain(ks[7], (N_A_LAYERS, D)),
        "conv_ln_b": nrm(ks[8], (N_A_LAYERS, D), 0.02),
        "conv_w_pw2": nrm(ks[9], (N_A_LAYERS, D, D), D ** -0.5),
        "conv_b_pw2": nrm(ks[10], (N_A_LAYERS, D), 0.02),
        "norm_kv": gain(ks[11], (D,)),
        "w_kv": nrm(ks[12], (D, 2 * KVD), D ** -0.5),
        "w_q": nrm(ks[13], (N_B_LAYERS, D, HD), D ** -0.5),
        "w_o": nrm(ks[14], (N_B_LAYERS, HD, D), HD ** -0.5),
        "sinks": nrm(ks[15], (N_B_LAYERS, N_HEADS), 0.5),
        "rel_bias": nrm(ks[16], (N_BUCKETS, N_HEADS), 0.5),
        "ffn_w_up": nrm(ks[17], (DEPTH, D, 2 * D_FF), D ** -0.5),
        "ffn_w_down": nrm(ks[18], (DEPTH, D_FF, D), D_FF ** -0.5),
        "norm_final": gain(ks[19], (D,)),
    }


def reference(x, norm_mix, norm_ffn, conv_w_pw1, conv_b_pw1, conv_w_dw, conv_b_dw,
              conv_ln_g, conv_ln_b, conv_w_pw2, conv_b_pw2, norm_kv, w_kv, w_q, w_o,
              sinks, rel_bias, ffn_w_up, ffn_w_down, norm_final):
    B, S = x.shape[0], x.shape[1]
    h = x
    k_shared = v_shared = None
    for l in range(DEPTH):
        if l < N_A_LAYERS:
            i = l
            h = h + conformer_conv(rmsnorm(h, norm_mix[l]), conv_w_pw1[i], conv_b_pw1[i],
                                   conv_w_dw[i], conv_b_dw[i], conv_ln_g[i], conv_ln_b[i],
                                   conv_w_pw2[i], conv_b_pw2[i])
        else:
            if l == N_A_LAYERS:
                kv = rmsnorm(h, norm_kv) @ w_kv
                k_flat, v_flat = jnp.split(kv, 2, axis=-1)
                k_shared = k_flat.reshape(B, S, N_KV_HEADS, HEAD_DIM)
                v_shared = v_flat.reshape(B, S, N_KV_HEADS, HEAD_DIM)
            j = l - N_A_LAYERS
            q = (rmsnorm(h, norm_mix[l]) @ w_q[j]).reshape(B, S, N_HEADS, HEAD_DIM)
            attn = banded_sink_attention(q, k_shared, v_shared, sinks[j], rel_bias)
            h = h + attn @ w_o[j]
        h = h + swiglu_ffn(rmsnorm(h, norm_ffn[l]), ffn_w_up[l], ffn_w_down[l])
    return rmsnorm(h, norm_final)
```

```python
import math
from contextlib import ExitStack

import numpy as np
import concourse.bass as bass
import concourse.mybir as mybir
from concourse.bass_utils import run_bass_kernel_spmd

F32 = mybir.dt.float32
BF16 = mybir.dt.bfloat16
AF = mybir.ActivationFunctionType
ALU = mybir.AluOpType

D = 1024
SEQ = 4096
NB = 8
DFF = 2816
NF = DFF // 128
KW = 31
EPS = 1e-6
TS = 1024
NS = TS // 512
NT = SEQ // TS
NSLOT = 4
SLOTW = 2816
MASKV = -30000.0
import os
DBG = {'pre': int(os.environ.get('KPRE', '1')), 'hf': int(os.environ.get('KHF', '2')), 'att': int(os.environ.get('KATT', '9')), 'sink': int(os.environ.get('KSINK', '1')), 'copy': int(os.environ.get('KCOPY', '1'))}

_ES = {F32: 4, BF16: 2}


def rect(ap):
    a = ap.ap
    pstride, npart = a[0]
    off = int(ap.offset)
    p0 = off // pstride
    f0 = off % pstride
    ext = 1
    for s, c in a[1:]:
        ext += (c - 1) * abs(s)
    es = _ES[ap.dtype]
    return (ap.tensor.name, p0, p0 + npart, f0 * es, (f0 + ext) * es)


class Op:
    __slots__ = ("eng", "idx", "fn", "deps", "dma", "sig", "need", "dslot")

    def __init__(self, eng, idx, fn, dma):
        self.eng = eng
        self.idx = idx
        self.fn = fn
        self.deps = set()
        self.dma = dma
        self.sig = None
        self.need = False
        self.dslot = None


class Ent:
    __slots__ = ("p0", "p1", "lo", "hi", "w", "rd", "rdma")

    def __init__(self, r, w):
        self.p0, self.p1, self.lo, self.hi = r[1], r[2], r[3], r[4]
        self.w = w
        self.rd = {}
        self.rdma = []


ENGS = ("pe", "act", "dve", "pool", "sp")
NDMASEM = 8


class Sched:
    def __init__(self, nc):
        self.nc = nc
        self.ops = {e: [] for e in ENGS}
        self.reg = {}
        self.ndma = {e: 0 for e in ENGS}

    def _hits(self, r):
        ents = self.reg.setdefault(r[0], [])
        p0, p1, lo, hi = r[1], r[2], r[3], r[4]
        return ents, [e for e in ents if e.p0 < p1 and p0 < e.p1 and e.lo < hi and lo < e.hi]

    def add(self, eng, fn, reads=(), writes=(), dma=False):
        op = Op(eng, len(self.ops[eng]), fn, dma)
        if dma:
            op.dslot = self.ndma[eng]
            self.ndma[eng] += 1
        self.ops[eng].append(op)
        raw = set()
        for a in reads:
            r = rect(a)
            ents, hit = self._hits(r)
            assert hit, f"read of never-written region {r}"
            for e in hit:
                if e.w is not None:
                    op.deps.add(e.w)
                    raw.add(e.w)
                if dma:
                    e.rdma.append(op)
                else:
                    e.rd[eng] = op
        for a in writes:
            r = rect(a)
            ents, hit = self._hits(r)
            for e in hit:
                if e.w is not None:
                    op.deps.add(e.w)
                for o in e.rd.values():
                    op.deps.add(o)
                for o in e.rdma:
                    op.deps.add(o)
            p0, p1, lo, hi = r[1], r[2], r[3], r[4]
            ents[:] = [e for e in ents if not (p0 <= e.p0 and e.p1 <= p1 and lo <= e.lo and e.hi <= hi)]
            ents.append(Ent(r, op))
        op.deps.discard(op)
        keep = set()
        for d in op.deps:
            if d.eng == eng and not d.dma and not dma:
                if eng == "pe" or d not in raw:
                    continue
            keep.add(d)
        op.deps = keep
        for d in op.deps:
            d.need = True
        return op

    def emit(self):
        nc = self.nc
        with ExitStack() as st:
            csem = {e: st.enter_context(nc.semaphore(f"c_{e}")) for e in ("pe", "act", "dve", "pool")}
            dsem = {e: [st.enter_context(nc.semaphore(f"d_{e}{i}")) for i in range(NDMASEM)]
                    for e in ("sp", "pool")}
            for e in ENGS:
                cnt = 0
                for op in self.ops[e]:
                    if op.dma:
                        i = op.dslot
                        op.sig = (dsem[e][i % NDMASEM], 16 * (i // NDMASEM + 1))
                    elif op.need:
                        cnt += 1
                        op.sig = (csem[e], cnt)
            block = st.enter_context(nc.Block())
            engobj = {"pe": "tensor", "act": "scalar", "dve": "vector", "pool": "gpsimd", "sp": "sync"}

            def run(ename, eng):
                waited = {}
                for op in self.ops[ename]:
                    ws = {}
                    for d in op.deps:
                        s, v = d.sig
                        k = id(s)
                        if waited.get(k, 0) >= v:
                            continue
                        if k not in ws or ws[k][1] < v:
                            ws[k] = (s, v)
                    if op.dma and op.dslot >= NDMASEM:
                        i = op.dslot
                        s = dsem[ename][i % NDMASEM]
                        v = 16 * (i // NDMASEM)
                        k = id(s)
                        if waited.get(k, 0) < v and (k not in ws or ws[k][1] < v):
                            ws[k] = (s, v)
                    for k, (s, v) in ws.items():
                        eng.wait_ge(s, v)
                        waited[k] = v
                    ins = op.fn(eng)
                    if op.dma:
                        ins.then_inc(op.sig[0], 16)
                    elif op.need:
                        ins.then_inc(op.sig[0], 1)
                n = self.ndma[ename]
                for j in range(min(n, NDMASEM)):
                    last = ((n - 1 - j) // NDMASEM) * NDMASEM + j
                    eng.wait_ge(dsem[ename][j], 16 * (last // NDMASEM + 1))

            for ename in ENGS:
                if not self.ops[ename]:
                    continue
                dec = getattr(block, engobj[ename])

                def body(eng, ename=ename):
                    run(ename, eng)
                dec(body)


def pv_layout():
    off = {}
    n = 0
    for name, cnt in (("norm_mix", 32), ("norm_ffn", 32), ("norm_kv", 8), ("norm_final", 8),
                      ("b_pw1", 32), ("b_dw", 16), ("ln_g", 16), ("ln_b", 16), ("b_pw2", 16), ("eps", 1)):
        off[name] = n
        n += cnt
    return off, n


PVO, NV = pv_layout()


def piece_list():
    P = []
    for l in range(4):
        if l < 2:
            P += [("pw1", l, i, 2048) for i in range(8)]
            P += [("pw2", l, i, 2048) for i in range(4)]
        else:
            if l == 2:
                P += [("wk", 0, i, 2048) for i in range(4)]
                P += [("wv", 0, i, 2048) for i in range(2)]
            P += [("wq", l - 2, i, 2048) for i in range(4)]
            P += [("wo", l - 2, i, 2048) for i in range(4)]
        P += [("up", l, i, 2048) for i in range(NF)]
        P += [("down", l, i, 2816) for i in range(8)]
    return P


PIECES = piece_list()
POFF = np.concatenate([[0], np.cumsum([p[3] for p in PIECES])]).astype(np.int64)
WTOT = int(POFF[-1])


def build_nc(nlayers=4, ntiles=NT, final=True, ffn_last=True):
    nc = bass.Bass("TRN2", target_bir_lowering=False)
    dr = lambda n, s, k: nc.dram_tensor(n, s, F32, kind=k).ap()
    xT = dr("xT", [D, SEQ], "ExternalInput")
    pvd = dr("pv", [128, NV], "ExternalInput")
    wdwd = dr("wdw", [128, 2 * 8 * KW], "ExternalInput")
    biasd = dr("biasT", [128, 2 * 16 * 128], "ExternalInput")
    skd = dr("sk", [2, 32], "ExternalInput")
    nmd = dr("nm", [2, 1], "ExternalInput")
    identd = dr("identf", [128, 128], "ExternalInput")
    wflat = dr("wflat", [128, WTOT], "ExternalInput")
    outT = dr("outT", [D, SEQ], "ExternalOutput")

    with ExitStack() as st:
        T = lambda n, s, d: st.enter_context(nc.sbuf_tensor(n, s, d))
        h = T("h", [128, 8, TS], F32)
        hn = T("hn", [128, 8, TS], BF16)
        big = T("big", [128, NF * TS // 2], F32)
        regc = T("regc", [128, 11008], F32)
        ring = T("ring", [128, NSLOT, SLOTW], BF16)
        ahalo = T("ahalo", [128, 2, 8, KW - 1], BF16)
        khalo = T("khalo", [128, 2, 4, 128], BF16)
        vhalo = T("vhalo", [128, 512], BF16)
        pv = T("pvs", [128, NV], F32)
        wdw = T("wdws", [128, 2 * 8 * KW], F32)
        tmp = T("tmp", [128, 6, 512], F32)
        statA = T("statA", [128, 8, 512], BF16)
        statB = T("statB", [128, 8, 512], BF16)
        pT = T("pT", [128, 3, 2, 512], BF16)
        ones_bf = T("ones_bf", [128, 128], BF16)
        ident_bf = T("ident_bf", [128, 128], BF16)
        skt = T("skt", [2, 32], F32)
        nmt = T("nmt", [2, 1], F32)
        et = T("et", [2, 32], F32)
        ehi = T("ehi", [2, 32], BF16)
        esel = T("esel", [2, 32], F32)
        esink = T("esink", [128, 32, 128], BF16)
        ps = st.enter_context(nc.psum_tensor("ps", [128, 8, 512], F32))

        bigb = big.bitcast(BF16)
        regb = regc.bitcast(BF16)
        act = bigb[:, 0:NF * TS].rearrange("p (f t) -> p f t", f=NF)
        y = big[:, 0:8 * TS].rearrange("p (c t) -> p c t", c=8)
        outb = y
        qT = bigb[:, 0:8 * TS].rearrange("p (c t) -> p c t", c=8)
        attnT = bigb[:, 8 * TS:16 * TS].rearrange("p (c t) -> p c t", c=8)
        AW = TS + KW - 1
        a_buf = regb[:, 0:8 * AW].rearrange("p (c t) -> p c t", c=8)
        diag = regb[:, 8 * AW:8 * AW + 2 * KW * 128].rearrange("p (s k j) -> p s k j", s=2, k=KW)
        biasT = regc[:, 0:4096].rearrange("p (j h q) -> p j h q", j=2, h=16)
        KWID = TS + 128
        kT = regb[:, 8192:8192 + 8 * KWID].rearrange("p (v g t) -> p v g t", v=2, g=4)
        v2 = regb[:, 8192 + 8 * KWID:8192 + 8 * KWID + (TS // 128 + 1) * 512].rearrange(
            "p (b n) -> p b n", n=512)
        assert 8192 + 8 * KWID + (TS // 128 + 1) * 512 <= 22016
        assert 8 * AW + 2 * KW * 128 <= 22016

        S = Sched(nc)
        add = S.add
        bank = [0]

        def I(eng, meth, reads, writes, *args, **kw):
            add(eng, lambda e: getattr(e, meth)(*args, **kw), reads=reads, writes=writes)

        def DMA(eng, out, in_, reads=(), writes=()):
            add(eng, lambda e: e.dma_start(out=out, in_=in_), reads=reads, writes=writes, dma=True)

        rot = list(range(8))

        def nb():
            b = rot[bank[0] % len(rot)]
            bank[0] += 1
            return b

        def reserve(n):
            return [rot.pop() for _ in range(n)]

        def release(bs):
            rot.extend(bs)

        pre = {"banks": None, "pend": [], "sqi": 0}

        def pre_begin():
            if DBG['pre']:
                pre["banks"] = reserve(2)

        def pre_update(oc, s, hh):
            if not DBG['pre']:
                return
            sq = statA[:, pre["sqi"] % 8, :]
            pre["sqi"] += 1
            I("act", "activation", [hh], [sq], out=sq, in_=hh, func=AF.Square)
            pre["pend"].append((pre["banks"][s], sq, oc == 0, oc == 7))
            pre_flush(2)

        def pre_flush(keep):
            while len(pre["pend"]) > keep:
                bnk, sq, first, last = pre["pend"].pop(0)
                mm(ps[:, bnk, :], ones_bf[:], sq, first, last)

        def mm(out, lhsT, rhs, start, stop):
            add("pe", lambda e: e.matmul(out, lhsT=lhsT, rhs=rhs, start=start, stop=stop),
                reads=[lhsT, rhs], writes=[out])

        def pcol(name, i):
            o = PVO[name] + i
            return pv[:, o:o + 1]

        epsc = pcol("eps", 0)

        DMA("sp", pv[:], pvd, writes=[pv[:]])
        DMA("sp", wdw[:], wdwd, writes=[wdw[:]])
        DMA("sp", skt[:], skd, writes=[skt[:]])
        DMA("sp", nmt[:], nmd, writes=[nmt[:]])
        DMA("pool", ident_bf[:], identd, writes=[ident_bf[:]])
        I("dve", "memset", [], [ones_bf[:]], ones_bf[:], 1.0)
        I("act", "activation", [skt[:]], [et[:]], out=et[:], in_=skt[:], func=AF.Exp)
        I("dve", "tensor_copy", [et[:]], [ehi[:]], out=ehi[:], in_=et[:])
        I("dve", "scalar_tensor_tensor", [ehi[:], nmt[:], et[:]], [esel[:]], out=esel[:], in0=ehi[:],
          scalar=nmt[:, 0:1], in1=et[:], op0=ALU.mult, op1=ALU.add)
        I("pool", "memset", [], [esink[:]], esink[:], 0.0)
        I("dve", "tensor_copy", [esel[:]], [esink[0:2, :, :]], out=esink[0:2, :, :],
          in_=esel[:].unsqueeze(2).to_broadcast([2, 32, 128]))

        seq = []
        for ti in range(ntiles):
            for pi, p in enumerate(PIECES):
                ok = (nlayers > 2) if p[0] in ("wk", "wv") else ((p[1] + (2 if p[0] in ("wq", "wo") else 0)) < nlayers)
                if p[0] in ("up", "down") and p[1] == nlayers - 1 and not ffn_last:
                    ok = False
                if p[0] == "wo" and DBG['att'] < 9:
                    ok = False
                if ok:
                    seq.append(pi)
        wstate = {"issued": 0, "used": 0}

        def issue_to(n):
            while wstate["issued"] < min(n, len(seq)):
                k = wstate["issued"]
                pi = seq[k]
                sl = k % NSLOT
                ln = PIECES[pi][3]
                o = int(POFF[pi])
                DMA("pool", ring[:, sl, 0:ln], wflat[:, o:o + ln], writes=[ring[:, sl, 0:ln]])
                wstate["issued"] += 1

        def wnext(kind):
            k = wstate["used"]
            assert PIECES[seq[k]][0] == kind, (PIECES[seq[k]], kind)
            issue_to(k + NSLOT)
            wstate["used"] += 1
            return ring[:, k % NSLOT, :]

        rstd = [tmp[:, 2, :], tmp[:, 3, :]]
        sigb = [tmp[:, 0, :], tmp[:, 1, :]]
        sigi = [0]

        def nsig():
            sigi[0] ^= 1
            return sigb[sigi[0]]

        def sr(s):
            return slice(s * 512, (s + 1) * 512)

        def rms_stats(s):
            if pre["banks"] is not None:
                pre_flush(0)
                b = pre["banks"][s]
                if s == NS - 1:
                    release(pre["banks"])
                    pre["banks"] = None
            else:
                for c in range(8):
                    I("act", "activation", [h[:, c, sr(s)]], [statA[:, c, :]],
                      out=statA[:, c, :], in_=h[:, c, sr(s)], func=AF.Square)
                b = nb()
                for c in range(8):
                    mm(ps[:, b, :], ones_bf[:], statA[:, c, :], c == 0, c == 7)
            I("act", "activation", [ps[:, b, :], epsc], [rstd[s]],
              out=rstd[s], in_=ps[:, b, :], func=AF.Ln, bias=epsc, scale=1.0 / D)
            I("act", "activation", [rstd[s]], [rstd[s]], out=rstd[s], in_=rstd[s], func=AF.Exp, scale=-0.5)

        def rms_apply(s, gname, gi, dst):
            for c in range(8):
                g = pcol(gname, gi * 8 + c)
                I("dve", "scalar_tensor_tensor", [h[:, c, sr(s)], g, rstd[s]], [dst[:, c, sr(s)]],
                  out=dst[:, c, sr(s)], in0=h[:, c, sr(s)], scalar=g, in1=rstd[s], op0=ALU.mult, op1=ALU.mult)

        def proj_resid(kind, src, nk, bias_name=None, bias_base=0):
            pre_begin()
            for q in range(4):
                w = wnext(kind)
                for o in range(2):
                    oc = 2 * q + o
                    for s in range(NS):
                        b = nb()
                        for kc in range(nk):
                            mm(ps[:, b, :], w[:, (o * nk + kc) * 128:(o * nk + kc + 1) * 128],
                               src[:, kc, sr(s)], kc == 0, kc == nk - 1)
                        hh = h[:, oc, sr(s)]
                        if bias_name is None:
                            I("dve", "tensor_tensor", [ps[:, b, :], hh], [hh], out=hh, in0=ps[:, b, :], in1=hh, op=ALU.add)
                        else:
                            bc = pcol(bias_name, bias_base + oc)
                            I("dve", "scalar_tensor_tensor", [ps[:, b, :], bc, hh], [hh],
                              out=hh, in0=ps[:, b, :], scalar=bc, in1=hh, op0=ALU.add, op1=ALU.add)
                        pre_update(oc, s, hh)

        def ffn(l):
            for s in range(NS):
                rms_stats(s)
                rms_apply(s, "norm_ffn", l, hn)
            for f in range(NF):
                w = wnext("up")
                for s in range(NS):
                    bG = nb()
                    bU = nb()
                    for kc in range(8):
                        mm(ps[:, bG, :], w[:, (kc * 2) * 128:(kc * 2 + 1) * 128], hn[:, kc, sr(s)], kc == 0, kc == 7)
                    for kc in range(8):
                        mm(ps[:, bU, :], w[:, (kc * 2 + 1) * 128:(kc * 2 + 2) * 128], hn[:, kc, sr(s)], kc == 0, kc == 7)
                    sg = nsig()
                    I("act", "activation", [ps[:, bG, :]], [sg], out=sg, in_=ps[:, bG, :], func=AF.Silu)
                    dst = act[:, f, sr(s)]
                    I("dve", "tensor_tensor", [sg, ps[:, bU, :]], [dst], out=dst, in0=sg, in1=ps[:, bU, :], op=ALU.mult)
            pre_begin()
            for oc in range(8):
                w = wnext("down")
                for s in range(NS):
                    b = nb()
                    for f in range(NF):
                        mm(ps[:, b, :], w[:, f * 128:(f + 1) * 128], act[:, f, sr(s)], f == 0, f == NF - 1)
                    hh = h[:, oc, sr(s)]
                    I("dve", "tensor_tensor", [ps[:, b, :], hh], [hh], out=hh, in0=ps[:, b, :], in1=hh, op=ALU.add)
                    pre_update(oc, s, hh)

        def conv_layer(l, ti):
            for s in range(NS):
                rms_stats(s)
                rms_apply(s, "norm_mix", l, hn)
            hal = a_buf[:, :, 0:KW - 1]
            if ti == 0:
                I("pool", "memset", [], [hal], hal, 0.0)
            else:
                I("pool", "tensor_copy", [ahalo[:, l, :, :]], [hal], out=hal, in_=ahalo[:, l, :, :])
            for oc in range(8):
                w = wnext("pw1")
                for s in range(NS):
                    bA = nb()
                    bG = nb()
                    for kc in range(8):
                        mm(ps[:, bA, :], w[:, (kc * 2) * 128:(kc * 2 + 1) * 128], hn[:, kc, sr(s)], kc == 0, kc == 7)
                    for kc in range(8):
                        mm(ps[:, bG, :], w[:, (kc * 2 + 1) * 128:(kc * 2 + 2) * 128], hn[:, kc, sr(s)], kc == 0, kc == 7)
                    sg = nsig()
                    bg = pcol("b_pw1", l * 16 + 8 + oc)
                    ba = pcol("b_pw1", l * 16 + oc)
                    I("act", "activation", [ps[:, bG, :], bg], [sg], out=sg, in_=ps[:, bG, :], func=AF.Sigmoid,
                      bias=bg, scale=1.0)
                    dst = a_buf[:, oc, KW - 1 + s * 512:KW - 1 + (s + 1) * 512]
                    I("dve", "scalar_tensor_tensor", [ps[:, bA, :], ba, sg], [dst],
                      out=dst, in0=ps[:, bA, :], scalar=ba, in1=sg, op0=ALU.add, op1=ALU.mult)
            for c in range(8):
                ds = c % 2
                for k in range(KW):
                    wc = wdw[:, (l * 8 + c) * KW + k:(l * 8 + c) * KW + k + 1]
                    I("dve", "tensor_scalar", [ident_bf[:], wc], [diag[:, ds, k, :]],
                      out=diag[:, ds, k, :], in0=ident_bf[:], scalar1=wc, scalar2=None, op0=ALU.mult)
                for s in range(NS):
                    b = nb()
                    for k in range(KW):
                        mm(ps[:, b, :], diag[:, ds, k, :], a_buf[:, c, s * 512 + k:s * 512 + k + 512], k == 0, k == KW - 1)
                    bd = pcol("b_dw", l * 8 + c)
                    I("act", "activation", [ps[:, b, :], bd], [y[:, c, sr(s)]],
                      out=y[:, c, sr(s)], in_=ps[:, b, :], func=AF.Identity, bias=bd, scale=1.0)
                    ysq = (statA if s == 0 else statB)[:, c, :]
                    I("act", "activation", [ps[:, b, :], bd], [ysq],
                      out=ysq, in_=ps[:, b, :], func=AF.Square, bias=bd, scale=1.0)
                    I("dve", "tensor_copy", [y[:, c, sr(s)]], [hn[:, c, sr(s)]],
                      out=hn[:, c, sr(s)], in_=y[:, c, sr(s)])
            ho = a_buf[:, :, TS:TS + KW - 1]
            I("pool", "tensor_copy", [ho], [ahalo[:, l, :, :]], out=ahalo[:, l, :, :], in_=ho)
            mean = tmp[:, 4, :]
            var = tmp[:, 5, :]
            for s in range(NS):
                bM = nb()
                bE = nb()
                sqb = statA if s == 0 else statB
                for c in range(8):
                    mm(ps[:, bM, :], ones_bf[:], hn[:, c, sr(s)], c == 0, c == 7)
                for c in range(8):
                    mm(ps[:, bE, :], ones_bf[:], sqb[:, c, :], c == 0, c == 7)
                I("dve", "tensor_scalar", [ps[:, bM, :]], [mean], out=mean, in0=ps[:, bM, :], scalar1=1.0 / D,
                  scalar2=None, op0=ALU.mult)
                I("dve", "tensor_tensor", [mean], [var], out=var, in0=mean, in1=mean, op=ALU.mult)
                I("dve", "scalar_tensor_tensor", [ps[:, bE, :], var], [var], out=var, in0=ps[:, bE, :],
                  scalar=1.0 / D, in1=var, op0=ALU.mult, op1=ALU.subtract)
                I("act", "activation", [var, epsc], [var], out=var, in_=var, func=AF.Ln, bias=epsc, scale=1.0)
                I("act", "activation", [var], [var], out=var, in_=var, func=AF.Exp, scale=-0.5)
                for c in range(8):
                    yy = y[:, c, sr(s)]
                    I("dve", "tensor_tensor", [yy, mean], [yy], out=yy, in0=yy, in1=mean, op=ALU.subtract)
                    I("dve", "tensor_tensor", [yy, var], [yy], out=yy, in0=yy, in1=var, op=ALU.mult)
                    g = pcol("ln_g", l * 8 + c)
                    bb = pcol("ln_b", l * 8 + c)
                    dst = hn[:, c, sr(s)]
                    I("act", "activation", [yy, g, bb], [dst], out=dst, in_=yy, func=AF.Silu, bias=bb, scale=g)
            proj_resid("pw2", hn, 8, "b_pw2", l * 8)

        def kv_proj(ti):
            for s in range(NS):
                rms_apply(s, "norm_kv", 0, hn)
            DMA("sp", regc[:, 0:4096], biasd, writes=[regc[:, 0:4096]])
            if ti > 0:
                I("pool", "tensor_copy", [khalo[:]], [kT[:, :, :, 0:128]], out=kT[:, :, :, 0:128], in_=khalo[:])
                I("pool", "tensor_copy", [vhalo[:]], [v2[:, 0, :]], out=v2[:, 0, :], in_=vhalo[:])
            for g in range(4):
                w = wnext("wk")
                for v in range(2):
                    for s in range(NS):
                        b = nb()
                        for kc in range(8):
                            mm(ps[:, b, :], w[:, (v * 8 + kc) * 128:(v * 8 + kc + 1) * 128], hn[:, kc, sr(s)],
                               kc == 0, kc == 7)
                        dst = kT[:, v, g, 128 + s * 512:128 + (s + 1) * 512]
                        I("act", "activation", [ps[:, b, :]], [dst], out=dst, in_=ps[:, b, :], func=AF.Copy)
            for vp in range(2):
                w = wnext("wv")
                for blk in range(TS // 128):
                    b = nb()
                    for kc in range(8):
                        mm(ps[:, b, 0:256], hn[:, kc, blk * 128:(blk + 1) * 128], w[:, kc * 256:(kc + 1) * 256],
                           kc == 0, kc == 7)
                    dst = v2[:, 1 + blk, vp * 256:(vp + 1) * 256]
                    I("dve", "tensor_copy", [ps[:, b, 0:256]], [dst], out=dst, in_=ps[:, b, 0:256])

        def attn_layer(l, ti):
            lj = l - 2
            for s in range(NS):
                if l != 2:
                    rms_stats(s)
                rms_apply(s, "norm_mix", l, hn)
            for qp in range(4):
                w = wnext("wq")
                for o in range(2):
                    c = 2 * qp + o
                    for s in range(NS):
                        b = nb()
                        for kc in range(8):
                            mm(ps[:, b, :], w[:, (o * 8 + kc) * 128:(o * 8 + kc + 1) * 128], hn[:, kc, sr(s)],
                               kc == 0, kc == 7)
                        dst = qT[:, c, sr(s)]
                        I("act", "activation", [ps[:, b, :]], [dst], out=dst, in_=ps[:, b, :], func=(AF.Copy if DBG['copy'] else AF.Identity))
            items = [(i, g) for i in range(TS // 128) for g in range(4)]
            ttb = [tmp[:, 0, :], tmp[:, 1, :]]
            rdb = [tmp[:, 4, :], tmp[:, 5, :]]
            cnt = {"t": 0}

            ttb4 = [tmp[:, 0, :], tmp[:, 1, :], tmp[:, 2, :], tmp[:, 3, :]]
            NI = len(items)
            jls = {}
            abk = {}

            def a1(n):
                i, g = items[n]
                q0 = i * 128
                jl = [1] if (ti == 0 and i == 0) else [0, 1]
                jls[n] = jl
                for j in jl:
                    kcol = (i + j) * 128
                    tt = ttb4[(n % 2) * 2 + j]
                    bS = nb()
                    for hf in range(2):
                        mm(ps[:, bS, hf * 256:hf * 256 + 256].rearrange("p (a b) -> p a b", a=2),
                           kT[:, hf, g, kcol:kcol + 128], qT[:, 2 * g:2 * g + 2, q0:q0 + 128], True, True)
                    bsrc = biasT[:, j, g * 4:(g + 1) * 4, :]
                    I("dve", "scalar_tensor_tensor", [ps[:, bS, :], bsrc], [tt],
                      out=tt.rearrange("p (a b) -> p a b", a=4), in0=ps[:, bS, :].rearrange("p (a b) -> p a b", a=4),
                      scalar=0.125, in1=bsrc, op0=ALU.mult, op1=ALU.add)

            def a2(n):
                for j in jls[n]:
                    tt = ttb4[(n % 2) * 2 + j]
                    dst = pT[:, n % 3, j, :]
                    I("act", "activation", [tt], [dst], out=dst, in_=tt, func=AF.Exp)

            def b1(n):
                i, g = items[n]
                jl = jls[n]
                bA = nb()
                bB = nb()
                abk[n] = bA
                for x, j in enumerate(jl):
                    mm(ps[:, bA, :], v2[:, i + j, g * 128:(g + 1) * 128], pT[:, n % 3, j, :], x == 0, x == len(jl) - 1)
                for x, j in enumerate(jl):
                    mm(ps[:, bB, :], ones_bf[:], pT[:, n % 3, j, :], x == 0, False)
                es = esink[:, lj * 16 + g * 4:lj * 16 + (g + 1) * 4, :]
                mm(ps[:, bB, :].rearrange("p (a b) -> p a b", a=4), ones_bf[:], es, False, True)
                rd = rdb[n % 2]
                I("act", "activation", [ps[:, bB, :]], [rd], out=rd, in_=ps[:, bB, :], func=AF.Ln)
                I("act", "activation", [rd], [rd], out=rd, in_=rd, func=AF.Exp, scale=-1.0)

            def b2(n):
                i, g = items[n]
                q0 = i * 128
                bA = abk[n]
                rd = rdb[n % 2]
                for hf in range(2):
                    pr = slice(hf * 64, hf * 64 + 64)
                    cs = slice(hf * 256, hf * 256 + 256)
                    dst = attnT[pr, 2 * g:2 * g + 2, q0:q0 + 128]
                    I("dve", "tensor_tensor", [ps[pr, bA, cs], rd[pr, cs]], [dst], out=dst,
                      in0=ps[pr, bA, cs].rearrange("p (a b) -> p a b", a=2),
                      in1=rd[pr, cs].rearrange("p (a b) -> p a b", a=2), op=ALU.mult)

            if DBG['att'] == 0:
                return
            for it in range(NI + 4):
                if it < NI:
                    a1(it)
                if 0 <= it - 1 < NI:
                    a2(it - 1)
                if 0 <= it - 3 < NI:
                    b1(it - 3)
                if 0 <= it - 4 < NI:
                    b2(it - 4)
            if DBG['att'] < 9:
                return
            if l == min(nlayers, 4) - 1 and ti < ntiles - 1:
                I("pool", "tensor_copy", [kT[:, :, :, TS:TS + 128]], [khalo[:]], out=khalo[:], in_=kT[:, :, :, TS:TS + 128])
                I("pool", "tensor_copy", [v2[:, TS // 128, :]], [vhalo[:]], out=vhalo[:], in_=v2[:, TS // 128, :])
            proj_resid("wo", attnT, 8)

        xTv = xT.rearrange("(c p) t -> p c t", p=128)
        oTv = outT.rearrange("(c p) t -> p c t", p=128)
        for ti in range(ntiles):
            t0 = ti * TS
            for c in range(8):
                DMA("sp", h[:, c, :], xTv[:, c, t0:t0 + TS], writes=[h[:, c, :]])
            for l in range(nlayers):
                if l < 2:
                    conv_layer(l, ti)
                else:
                    if l == 2:
                        for s in range(NS):
                            rms_stats(s)
                        kv_proj(ti)
                    attn_layer(l, ti)
                if ffn_last or l < nlayers - 1:
                    ffn(l)
            if final:
                for s in range(NS):
                    rms_stats(s)
                    rms_apply(s, "norm_final", 0, outb)
                src = outb
            else:
                src = h
            for c in range(8):
                DMA("sp", oTv[:, c, t0:t0 + TS], src[:, c, :], reads=[src[:, c, :]])
        assert wstate["used"] == len(seq), (wstate, len(seq))
        S.emit()
    return nc


def t5_bucket_table():
    d = np.arange(128)
    max_exact = 16
    lr = np.log(np.maximum(d, 1).astype(np.float32) / max_exact) / math.log(128 / max_exact)
    large = max_exact + (lr * (32 - max_exact)).astype(np.int32)
    large = np.minimum(large, 31)
    return np.where(d < max_exact, d, large)


def cols(v):
    v = np.asarray(v, np.float32).reshape(-1, 128)
    return np.ascontiguousarray(v.T)


def host_prep(inp):
    f = lambda k: np.asarray(inp[k], np.float32)
    pv = np.zeros((128, NV), np.float32)

    def put(name, arr):
        c = cols(arr)
        pv[:, PVO[name]:PVO[name] + c.shape[1]] = c
    put("norm_mix", f("norm_mix").reshape(-1))
    put("norm_ffn", f("norm_ffn").reshape(-1))
    put("norm_kv", f("norm_kv"))
    put("norm_final", f("norm_final"))
    put("b_pw1", f("conv_b_pw1").reshape(-1))
    put("b_dw", f("conv_b_dw").reshape(-1))
    put("ln_g", f("conv_ln_g").reshape(-1))
    put("ln_b", f("conv_ln_b").reshape(-1))
    put("b_pw2", f("conv_b_pw2").reshape(-1))
    pv[:, PVO["eps"]] = EPS
    wdw = f("conv_w_dw").reshape(2, KW, 8, 128).transpose(3, 0, 2, 1).reshape(128, 2 * 8 * KW)
    wdw = np.ascontiguousarray(wdw)
    hord = []
    for g in range(4):
        hord += [4 * g, 4 * g + 2, 4 * g + 1, 4 * g + 3]
    bt = t5_bucket_table()
    rb = f("rel_bias")[:, hord]
    kk = np.arange(128)[:, None]
    qq = np.arange(128)[None, :]
    biasT = np.full((128, 2, 16, 128), MASKV, np.float32)
    for j in range(2):
        dist = qq - kk + (128 if j == 0 else 0)
        ok = (dist >= 0) & (dist < 128)
        tab = rb[bt[np.clip(dist, 0, 127)]]
        tab = np.where(ok[:, :, None], tab, np.float32(MASKV)).astype(np.float32)
        biasT[:, j] = tab.transpose(0, 2, 1)
    biasT = np.ascontiguousarray(biasT.reshape(128, 2 * 16 * 128))
    sk = np.ascontiguousarray(np.broadcast_to(f("sinks")[:, hord].reshape(1, 32), (2, 32))).astype(np.float32)
    nm = np.array([[0.0], [-1.0]], np.float32)
    identf = np.eye(128, dtype=np.float32)
    wflat = np.empty((128, WTOT), np.float32)
    w1 = f("conv_w_pw1").reshape(2, 8, 128, 2, 8, 128)
    w2 = f("conv_w_pw2").reshape(2, 8, 128, 8, 128)
    wup = f("ffn_w_up").reshape(4, 8, 128, 2, NF, 128)
    wdn = f("ffn_w_down").reshape(4, NF, 128, 8, 128)
    wq = f("w_q").reshape(2, 8, 128, 8, 128)
    wo = f("w_o").reshape(2, 8, 128, 8, 128)
    wkv = f("w_kv")
    wk0 = wkv[:, 0:256].reshape(8, 128, 4, 64)
    zz = np.zeros_like(wk0)
    wk = np.stack([np.concatenate([wk0, zz], axis=-1), np.concatenate([zz, wk0], axis=-1)], axis=2)
    wv = wkv[:, 256:512].reshape(8, 128, 4, 64)
    wv = np.concatenate([wv, wv], axis=-1)
    for pi, (kind, l, i, ln) in enumerate(PIECES):
        o = int(POFF[pi])
        if kind == "pw1":
            a = w1[l, :, :, :, i, :].transpose(1, 0, 2, 3)
        elif kind == "pw2":
            a = w2[l, :, :, 2 * i:2 * i + 2, :].transpose(1, 2, 0, 3)
        elif kind == "up":
            a = wup[l, :, :, :, i, :].transpose(1, 0, 2, 3)
        elif kind == "down":
            a = wdn[l, :, :, i, :].transpose(1, 0, 2)
        elif kind == "wq":
            a = wq[l, :, :, 2 * i:2 * i + 2, :].transpose(1, 2, 0, 3)
        elif kind == "wo":
            a = wo[l, :, :, 2 * i:2 * i + 2, :].transpose(1, 2, 0, 3)
        elif kind == "wk":
            a = wk[:, :, :, i, :].transpose(1, 2, 0, 3)
        elif kind == "wv":
            a = wv[:, :, 2 * i:2 * i + 2, :].transpose(1, 0, 2, 3)
        wflat[:, o:o + ln] = a.reshape(128, ln)
    return dict(pv=pv, wdw=wdw, biasT=biasT, sk=sk, nm=nm, identf=identf, wflat=wflat)


_NC_CACHE = {}


def run(inputs, nlayers=4, ntiles=NT, final=True):
    key = (nlayers, ntiles, final)
    if key not in _NC_CACHE:
        _NC_CACHE[key] = build_nc(nlayers, ntiles, final)
    nc = _NC_CACHE[key]
    shared = host_prep(inputs)
    x = np.asarray(inputs["x"], np.float32)
    in_maps = []
    for b in range(NB):
        m = dict(shared)
        m["xT"] = np.ascontiguousarray(x[b].T)
        in_maps.append(m)
    res = run_bass_kernel_spmd(nc, in_maps, core_ids=list(range(NB)))
    out = np.stack([np.ascontiguousarray(r["outT"].T) for r in res.results], axis=0)
    return out.astype(np.float32)


def kernel(**inputs):
    return run(inputs)
```

```python
import math
from contextlib import ExitStack

import numpy as np
import concourse.bass as bass
import concourse.mybir as mybir
from concourse.bass_utils import run_bass_kernel_spmd

F32 = mybir.dt.float32
BF16 = mybir.dt.bfloat16
AF = mybir.ActivationFunctionType
ALU = mybir.AluOpType

D = 1024
SEQ = 4096
NB = 8
DFF = 2816
NF = DFF // 128
KW = 31
EPS = 1e-6
TS = 1024
NS = TS // 512
NT = SEQ // TS
NSLOT = 4
SLOTW = 2816
MASKV = -30000.0
import os
DBG = {'pre': int(os.environ.get('KPRE', '0')), 'hf': int(os.environ.get('KHF', '2')), 'att': int(os.environ.get('KATT', '9')), 'sink': int(os.environ.get('KSINK', '1')), 'copy': int(os.environ.get('KCOPY', '1'))}

_ES = {F32: 4, BF16: 2}


def rect(ap):
    a = ap.ap
    pstride, npart = a[0]
    off = int(ap.offset)
    p0 = off // pstride
    f0 = off % pstride
    ext = 1
    for s, c in a[1:]:
        ext += (c - 1) * abs(s)
    es = _ES[ap.dtype]
    return (ap.tensor.name, p0, p0 + npart, f0 * es, (f0 + ext) * es)


class Op:
    __slots__ = ("eng", "idx", "fn", "deps", "dma", "sig", "need", "dslot", "seq")

    def __init__(self, eng, idx, fn, dma):
        self.eng = eng
        self.idx = idx
        self.fn = fn
        self.deps = set()
        self.dma = dma
        self.sig = None
        self.need = False
        self.dslot = None


class Ent:
    __slots__ = ("p0", "p1", "lo", "hi", "w", "rd", "rdma")

    def __init__(self, r, w):
        self.p0, self.p1, self.lo, self.hi = r[1], r[2], r[3], r[4]
        self.w = w
        self.rd = {}
        self.rdma = []


ENGS = ("pe", "act", "dve", "pool", "sp")
NDMASEM = 8
PE_HOIST = 10


class Sched:
    def __init__(self, nc):
        self.nc = nc
        self.ops = {e: [] for e in ENGS}
        self.reg = {}
        self.ndma = {e: 0 for e in ENGS}

    def _hits(self, r):
        ents = self.reg.setdefault(r[0], [])
        p0, p1, lo, hi = r[1], r[2], r[3], r[4]
        return ents, [e for e in ents if e.p0 < p1 and p0 < e.p1 and e.lo < hi and lo < e.hi]

    def add(self, eng, fn, reads=(), writes=(), dma=False):
        op = Op(eng, len(self.ops[eng]), fn, dma)
        self.nseq = getattr(self, "nseq", 0) + 1
        op.seq = self.nseq
        if dma:
            op.dslot = self.ndma[eng]
            self.ndma[eng] += 1
        self.ops[eng].append(op)
        raw = set()
        for a in reads:
            r = rect(a)
            ents, hit = self._hits(r)
            assert hit, f"read of never-written region {r}"
            for e in hit:
                if e.w is not None:
                    op.deps.add(e.w)
                    raw.add(e.w)
                if dma:
                    e.rdma.append(op)
                else:
                    e.rd[eng] = op
        for a in writes:
            r = rect(a)
            ents, hit = self._hits(r)
            for e in hit:
                if e.w is not None:
                    op.deps.add(e.w)
                for o in e.rd.values():
                    op.deps.add(o)
                for o in e.rdma:
                    op.deps.add(o)
            p0, p1, lo, hi = r[1], r[2], r[3], r[4]
            ents[:] = [e for e in ents if not (p0 <= e.p0 and e.p1 <= p1 and lo <= e.lo and e.hi <= hi)]
            ents.append(Ent(r, op))
        op.deps.discard(op)
        keep = set()
        for d in op.deps:
            if d.eng == eng and not d.dma and not dma:
                if eng == "pe" or d not in raw:
                    continue
            keep.add(d)
        op.deps = keep
        for d in op.deps:
            d.need = True
        return op

    def emit(self):
        nc = self.nc
        with ExitStack() as st:
            csem = {e: st.enter_context(nc.semaphore(f"c_{e}")) for e in ("pe", "act", "dve", "pool")}
            dsem = {e: [st.enter_context(nc.semaphore(f"d_{e}{i}")) for i in range(NDMASEM)]
                    for e in ("sp", "pool")}
            for e in ENGS:
                cnt = 0
                for op in self.ops[e]:
                    if op.dma:
                        i = op.dslot
                        op.sig = (dsem[e][i % NDMASEM], 16 * (i // NDMASEM + 1))
                    elif op.need:
                        cnt += 1
                        op.sig = (csem[e], cnt)
            block = st.enter_context(nc.Block())
            engobj = {"pe": "tensor", "act": "scalar", "dve": "vector", "pool": "gpsimd", "sp": "sync"}

            def run(ename, eng):
                waited = {}
                oplist = self.ops[ename]
                H = PE_HOIST if ename == "pe" else 0
                for oi, op in enumerate(oplist):
                    ws = {}
                    for o2 in oplist[oi:oi + 1 + H]:
                        for d in o2.deps:
                            if o2 is not op and d.seq >= op.seq:
                                continue
                            s, v = d.sig
                            k = id(s)
                            if waited.get(k, 0) >= v:
                                continue
                            if k not in ws or ws[k][1] < v:
                                ws[k] = (s, v)
                    if op.dma and op.dslot >= NDMASEM:
                        i = op.dslot
                        s = dsem[ename][i % NDMASEM]
                        v = 16 * (i // NDMASEM)
                        k = id(s)
                        if waited.get(k, 0) < v and (k not in ws or ws[k][1] < v):
                            ws[k] = (s, v)
                    for k, (s, v) in ws.items():
                        eng.wait_ge(s, v)
                        waited[k] = v
                    ins = op.fn(eng)
                    if op.dma:
                        ins.then_inc(op.sig[0], 16)
                    elif op.need:
                        ins.then_inc(op.sig[0], 1)
                n = self.ndma[ename]
                for j in range(min(n, NDMASEM)):
                    last = ((n - 1 - j) // NDMASEM) * NDMASEM + j
                    eng.wait_ge(dsem[ename][j], 16 * (last // NDMASEM + 1))

            for ename in ENGS:
                if not self.ops[ename]:
                    continue
                dec = getattr(block, engobj[ename])

                def body(eng, ename=ename):
                    run(ename, eng)
                dec(body)


def pv_layout():
    off = {}
    n = 0
    for name, cnt in (("norm_mix", 32), ("norm_ffn", 32), ("norm_kv", 8), ("norm_final", 8),
                      ("b_pw1", 32), ("b_dw", 16), ("ln_g", 16), ("ln_b", 16), ("b_pw2", 16), ("eps", 1)):
        off[name] = n
        n += cnt
    return off, n


PVO, NV = pv_layout()


def piece_list():
    P = []
    for l in range(4):
        if l < 2:
            P += [("pw1", l, i, 2048) for i in range(8)]
            P += [("pw2", l, i, 2048) for i in range(4)]
        else:
            if l == 2:
                P += [("wk", 0, i, 2048) for i in range(4)]
                P += [("wv", 0, i, 2048) for i in range(2)]
            P += [("wq", l - 2, i, 2048) for i in range(4)]
            P += [("wo", l - 2, i, 2048) for i in range(4)]
        P += [("up", l, i, 2048) for i in range(NF)]
        P += [("down", l, i, 2816) for i in range(8)]
    return P


PIECES = piece_list()
POFF = np.concatenate([[0], np.cumsum([p[3] for p in PIECES])]).astype(np.int64)
WTOT = int(POFF[-1])


def build_nc(nlayers=4, ntiles=NT, final=True, ffn_last=True):
    nc = bass.Bass("TRN2", target_bir_lowering=False)
    dr = lambda n, s, k: nc.dram_tensor(n, s, F32, kind=k).ap()
    xT = dr("xT", [D, SEQ], "ExternalInput")
    pvd = dr("pv", [128, NV], "ExternalInput")
    wdwd = dr("wdw", [128, 2 * 8 * KW], "ExternalInput")
    biasd = dr("biasT", [128, 2 * 16 * 128], "ExternalInput")
    skd = dr("sk", [2, 32], "ExternalInput")
    nmd = dr("nm", [2, 1], "ExternalInput")
    identd = dr("identf", [128, 128], "ExternalInput")
    wflat = dr("wflat", [128, WTOT], "ExternalInput")
    outT = dr("outT", [D, SEQ], "ExternalOutput")

    with ExitStack() as st:
        T = lambda n, s, d: st.enter_context(nc.sbuf_tensor(n, s, d))
        h = T("h", [128, 8, TS], F32)
        hn = T("hn", [128, 8, TS], BF16)
        big = T("big", [128, NF * TS // 2], F32)
        regc = T("regc", [128, 11008], F32)
        ring = T("ring", [128, NSLOT, SLOTW], BF16)
        ahalo = T("ahalo", [128, 2, 8, KW - 1], BF16)
        khalo = T("khalo", [128, 2, 4, 128], BF16)
        vhalo = T("vhalo", [128, 512], BF16)
        pv = T("pvs", [128, NV], F32)
        wdw = T("wdws", [128, 2 * 8 * KW], F32)
        tmp = T("tmp", [128, 6, 512], F32)
        statA = T("statA", [128, 8, 512], BF16)
        statB = T("statB", [128, 8, 512], BF16)
        pT = T("pT", [128, 3, 2, 512], BF16)
        ones_bf = T("ones_bf", [128, 128], BF16)
        ident_bf = T("ident_bf", [128, 128], BF16)
        skt = T("skt", [2, 32], F32)
        nmt = T("nmt", [2, 1], F32)
        et = T("et", [2, 32], F32)
        ehi = T("ehi", [2, 32], BF16)
        esel = T("esel", [2, 32], F32)
        esink = T("esink", [128, 32, 128], BF16)
        ps = st.enter_context(nc.psum_tensor("ps", [128, 8, 512], F32))

        bigb = big.bitcast(BF16)
        regb = regc.bitcast(BF16)
        act = bigb[:, 0:NF * TS].rearrange("p (f t) -> p f t", f=NF)
        y = big[:, 0:8 * TS].rearrange("p (c t) -> p c t", c=8)
        outb = y
        qT = bigb[:, 0:8 * TS].rearrange("p (c t) -> p c t", c=8)
        attnT = bigb[:, 8 * TS:16 * TS].rearrange("p (c t) -> p c t", c=8)
        AW = TS + KW - 1
        a_buf = regb[:, 0:8 * AW].rearrange("p (c t) -> p c t", c=8)
        diag = regb[:, 8 * AW:8 * AW + 2 * KW * 128].rearrange("p (s k j) -> p s k j", s=2, k=KW)
        biasT = regc[:, 0:4096].rearrange("p (j h q) -> p j h q", j=2, h=16)
        KWID = TS + 128
        kT = regb[:, 8192:8192 + 8 * KWID].rearrange("p (v g t) -> p v g t", v=2, g=4)
        v2 = regb[:, 8192 + 8 * KWID:8192 + 8 * KWID + (TS // 128 + 1) * 512].rearrange(
            "p (b n) -> p b n", n=512)
        assert 8192 + 8 * KWID + (TS // 128 + 1) * 512 <= 22016
        assert 8 * AW + 2 * KW * 128 <= 22016

        S = Sched(nc)
        add = S.add
        bank = [0]

        def I(eng, meth, reads, writes, *args, **kw):
            add(eng, lambda e: getattr(e, meth)(*args, **kw), reads=reads, writes=writes)

        def DMA(eng, out, in_, reads=(), writes=()):
            add(eng, lambda e: e.dma_start(out=out, in_=in_), reads=reads, writes=writes, dma=True)

        rot = list(range(8))

        def nb():
            b = rot[bank[0] % len(rot)]
            bank[0] += 1
            return b

        def reserve(n):
            return [rot.pop() for _ in range(n)]

        def release(bs):
            rot.extend(bs)

        pre = {"banks": None, "pend": [], "sqi": 0}

        def pre_begin():
            if DBG['pre']:
                pre["banks"] = reserve(2)

        def pre_update(oc, s, hh):
            if not DBG['pre']:
                return
            sq = statA[:, pre["sqi"] % 8, :]
            pre["sqi"] += 1
            I("act", "activation", [hh], [sq], out=sq, in_=hh, func=AF.Square)
            pre["pend"].append((pre["banks"][s], sq, oc == 0, oc == 7))
            pre_flush(2)

        def pre_flush(keep):
            while len(pre["pend"]) > keep:
                bnk, sq, first, last = pre["pend"].pop(0)
                mm(ps[:, bnk, :], ones_bf[:], sq, first, last)

        def mm(out, lhsT, rhs, start, stop):
            add("pe", lambda e: e.matmul(out, lhsT=lhsT, rhs=rhs, start=start, stop=stop),
                reads=[lhsT, rhs], writes=[out])

        def pcol(name, i):
            o = PVO[name] + i
            return pv[:, o:o + 1]

        epsc = pcol("eps", 0)

        DMA("sp", pv[:], pvd, writes=[pv[:]])
        DMA("sp", wdw[:], wdwd, writes=[wdw[:]])
        DMA("sp", skt[:], skd, writes=[skt[:]])
        DMA("sp", nmt[:], nmd, writes=[nmt[:]])
        DMA("pool", ident_bf[:], identd, writes=[ident_bf[:]])
        I("dve", "memset", [], [ones_bf[:]], ones_bf[:], 1.0)
        I("act", "activation", [skt[:]], [et[:]], out=et[:], in_=skt[:], func=AF.Exp)
        I("dve", "tensor_copy", [et[:]], [ehi[:]], out=ehi[:], in_=et[:])
        I("dve", "scalar_tensor_tensor", [ehi[:], nmt[:], et[:]], [esel[:]], out=esel[:], in0=ehi[:],
          scalar=nmt[:, 0:1], in1=et[:], op0=ALU.mult, op1=ALU.add)
        I("pool", "memset", [], [esink[:]], esink[:], 0.0)
        I("dve", "tensor_copy", [esel[:]], [esink[0:2, :, :]], out=esink[0:2, :, :],
          in_=esel[:].unsqueeze(2).to_broadcast([2, 32, 128]))

        seq = []
        for ti in range(ntiles):
            for pi, p in enumerate(PIECES):
                ok = (nlayers > 2) if p[0] in ("wk", "wv") else ((p[1] + (2 if p[0] in ("wq", "wo") else 0)) < nlayers)
                if p[0] in ("up", "down") and p[1] == nlayers - 1 and not ffn_last:
                    ok = False
                if p[0] == "wo" and DBG['att'] < 9:
                    ok = False
                if ok:
                    seq.append(pi)
        wstate = {"issued": 0, "used": 0}

        def issue_to(n):
            while wstate["issued"] < min(n, len(seq)):
                k = wstate["issued"]
                pi = seq[k]
                sl = k % NSLOT
                ln = PIECES[pi][3]
                o = int(POFF[pi])
                DMA("pool", ring[:, sl, 0:ln], wflat[:, o:o + ln], writes=[ring[:, sl, 0:ln]])
                wstate["issued"] += 1

        def wnext(kind):
            k = wstate["used"]
            assert PIECES[seq[k]][0] == kind, (PIECES[seq[k]], kind)
            issue_to(k + NSLOT)
            wstate["used"] += 1
            return ring[:, k % NSLOT, :]

        rstd = [tmp[:, 2, :], tmp[:, 3, :]]
        sigb = [tmp[:, 0, :], tmp[:, 1, :]]
        sigi = [0]

        def nsig():
            sigi[0] ^= 1
            return sigb[sigi[0]]

        def sr(s):
            return slice(s * 512, (s + 1) * 512)

        def rms_stats(s):
            if pre["banks"] is not None:
                pre_flush(0)
                b = pre["banks"][s]
                if s == NS - 1:
                    release(pre["banks"])
                    pre["banks"] = None
            else:
                for c in range(8):
                    I("act", "activation", [h[:, c, sr(s)]], [statA[:, c, :]],
                      out=statA[:, c, :], in_=h[:, c, sr(s)], func=AF.Square)
                b = nb()
                for c in range(8):
                    mm(ps[:, b, :], ones_bf[:], statA[:, c, :], c == 0, c == 7)
            I("act", "activation", [ps[:, b, :], epsc], [rstd[s]],
              out=rstd[s], in_=ps[:, b, :], func=AF.Ln, bias=epsc, scale=1.0 / D)
            I("act", "activation", [rstd[s]], [rstd[s]], out=rstd[s], in_=rstd[s], func=AF.Exp, scale=-0.5)

        def rms_apply(s, gname, gi, dst):
            for c in range(8):
                g = pcol(gname, gi * 8 + c)
                I("dve", "scalar_tensor_tensor", [h[:, c, sr(s)], g, rstd[s]], [dst[:, c, sr(s)]],
                  out=dst[:, c, sr(s)], in0=h[:, c, sr(s)], scalar=g, in1=rstd[s], op0=ALU.mult, op1=ALU.mult)

        def proj_resid(kind, src, nk, bias_name=None, bias_base=0):
            pre_begin()
            for q in range(4):
                w = wnext(kind)
                for o in range(2):
                    oc = 2 * q + o
                    for s in range(NS):
                        b = nb()
                        for kc in range(nk):
                            mm(ps[:, b, :], w[:, (o * nk + kc) * 128:(o * nk + kc + 1) * 128],
                               src[:, kc, sr(s)], kc == 0, kc == nk - 1)
                        hh = h[:, oc, sr(s)]
                        if bias_name is None:
                            I("dve", "tensor_tensor", [ps[:, b, :], hh], [hh], out=hh, in0=ps[:, b, :], in1=hh, op=ALU.add)
                        else:
                            bc = pcol(bias_name, bias_base + oc)
                            I("dve", "scalar_tensor_tensor", [ps[:, b, :], bc, hh], [hh],
                              out=hh, in0=ps[:, b, :], scalar=bc, in1=hh, op0=ALU.add, op1=ALU.add)
                        pre_update(oc, s, hh)

        def ffn(l):
            for s in range(NS):
                rms_stats(s)
                rms_apply(s, "norm_ffn", l, hn)
            for f in range(NF):
                w = wnext("up")
                for s in range(NS):
                    bG = nb()
                    bU = nb()
                    for kc in range(8):
                        mm(ps[:, bG, :], w[:, (kc * 2) * 128:(kc * 2 + 1) * 128], hn[:, kc, sr(s)], kc == 0, kc == 7)
                    for kc in range(8):
                        mm(ps[:, bU, :], w[:, (kc * 2 + 1) * 128:(kc * 2 + 2) * 128], hn[:, kc, sr(s)], kc == 0, kc == 7)
                    sg = nsig()
                    I("act", "activation", [ps[:, bG, :]], [sg], out=sg, in_=ps[:, bG, :], func=AF.Silu)
                    dst = act[:, f, sr(s)]
                    I("dve", "tensor_tensor", [sg, ps[:, bU, :]], [dst], out=dst, in0=sg, in1=ps[:, bU, :], op=ALU.mult)
            pre_begin()
            for oc in range(8):
                w = wnext("down")
                for s in range(NS):
                    b = nb()
                    for f in range(NF):
                        mm(ps[:, b, :], w[:, f * 128:(f + 1) * 128], act[:, f, sr(s)], f == 0, f == NF - 1)
                    hh = h[:, oc, sr(s)]
                    I("dve", "tensor_tensor", [ps[:, b, :], hh], [hh], out=hh, in0=ps[:, b, :], in1=hh, op=ALU.add)
                    pre_update(oc, s, hh)

        def conv_layer(l, ti):
            for s in range(NS):
                rms_stats(s)
                rms_apply(s, "norm_mix", l, hn)
            hal = a_buf[:, :, 0:KW - 1]
            if ti == 0:
                I("pool", "memset", [], [hal], hal, 0.0)
            else:
                I("pool", "tensor_copy", [ahalo[:, l, :, :]], [hal], out=hal, in_=ahalo[:, l, :, :])
            for oc in range(8):
                w = wnext("pw1")
                for s in range(NS):
                    bA = nb()
                    bG = nb()
                    for kc in range(8):
                        mm(ps[:, bA, :], w[:, (kc * 2) * 128:(kc * 2 + 1) * 128], hn[:, kc, sr(s)], kc == 0, kc == 7)
                    for kc in range(8):
                        mm(ps[:, bG, :], w[:, (kc * 2 + 1) * 128:(kc * 2 + 2) * 128], hn[:, kc, sr(s)], kc == 0, kc == 7)
                    sg = nsig()
                    bg = pcol("b_pw1", l * 16 + 8 + oc)
                    ba = pcol("b_pw1", l * 16 + oc)
                    I("act", "activation", [ps[:, bG, :], bg], [sg], out=sg, in_=ps[:, bG, :], func=AF.Sigmoid,
                      bias=bg, scale=1.0)
                    dst = a_buf[:, oc, KW - 1 + s * 512:KW - 1 + (s + 1) * 512]
                    I("dve", "scalar_tensor_tensor", [ps[:, bA, :], ba, sg], [dst],
                      out=dst, in0=ps[:, bA, :], scalar=ba, in1=sg, op0=ALU.add, op1=ALU.mult)
            for c in range(8):
                ds = c % 2
                for k in range(KW):
                    wc = wdw[:, (l * 8 + c) * KW + k:(l * 8 + c) * KW + k + 1]
                    I("dve", "tensor_scalar", [ident_bf[:], wc], [diag[:, ds, k, :]],
                      out=diag[:, ds, k, :], in0=ident_bf[:], scalar1=wc, scalar2=None, op0=ALU.mult)
                for s in range(NS):
                    b = nb()
                    for k in range(KW):
                        mm(ps[:, b, :], diag[:, ds, k, :], a_buf[:, c, s * 512 + k:s * 512 + k + 512], k == 0, k == KW - 1)
                    bd = pcol("b_dw", l * 8 + c)
                    I("act", "activation", [ps[:, b, :], bd], [y[:, c, sr(s)]],
                      out=y[:, c, sr(s)], in_=ps[:, b, :], func=AF.Identity, bias=bd, scale=1.0)
                    ysq = (statA if s == 0 else statB)[:, c, :]
                    I("act", "activation", [ps[:, b, :], bd], [ysq],
                      out=ysq, in_=ps[:, b, :], func=AF.Square, bias=bd, scale=1.0)
                    I("dve", "tensor_copy", [y[:, c, sr(s)]], [hn[:, c, sr(s)]],
                      out=hn[:, c, sr(s)], in_=y[:, c, sr(s)])
            ho = a_buf[:, :, TS:TS + KW - 1]
            I("pool", "tensor_copy", [ho], [ahalo[:, l, :, :]], out=ahalo[:, l, :, :], in_=ho)
            mean = tmp[:, 4, :]
            var = tmp[:, 5, :]
            for s in range(NS):
                bM = nb()
                bE = nb()
                sqb = statA if s == 0 else statB
                for c in range(8):
                    mm(ps[:, bM, :], ones_bf[:], hn[:, c, sr(s)], c == 0, c == 7)
                for c in range(8):
                    mm(ps[:, bE, :], ones_bf[:], sqb[:, c, :], c == 0, c == 7)
                I("dve", "tensor_scalar", [ps[:, bM, :]], [mean], out=mean, in0=ps[:, bM, :], scalar1=1.0 / D,
                  scalar2=None, op0=ALU.mult)
                I("dve", "tensor_tensor", [mean], [var], out=var, in0=mean, in1=mean, op=ALU.mult)
                I("dve", "scalar_tensor_tensor", [ps[:, bE, :], var], [var], out=var, in0=ps[:, bE, :],
                  scalar=1.0 / D, in1=var, op0=ALU.mult, op1=ALU.subtract)
                I("act", "activation", [var, epsc], [var], out=var, in_=var, func=AF.Ln, bias=epsc, scale=1.0)
                I("act", "activation", [var], [var], out=var, in_=var, func=AF.Exp, scale=-0.5)
                for c in range(8):
                    yy = y[:, c, sr(s)]
                    I("dve", "tensor_tensor", [yy, mean], [yy], out=yy, in0=yy, in1=mean, op=ALU.subtract)
                    I("dve", "tensor_tensor", [yy, var], [yy], out=yy, in0=yy, in1=var, op=ALU.mult)
                    g = pcol("ln_g", l * 8 + c)
                    bb = pcol("ln_b", l * 8 + c)
                    dst = hn[:, c, sr(s)]
                    I("act", "activation", [yy, g, bb], [dst], out=dst, in_=yy, func=AF.Silu, bias=bb, scale=g)
            proj_resid("pw2", hn, 8, "b_pw2", l * 8)

        def kv_proj(ti):
            for s in range(NS):
                rms_apply(s, "norm_kv", 0, hn)
            DMA("sp", regc[:, 0:4096], biasd, writes=[regc[:, 0:4096]])
            if ti > 0:
                I("pool", "tensor_copy", [khalo[:]], [kT[:, :, :, 0:128]], out=kT[:, :, :, 0:128], in_=khalo[:])
                I("pool", "tensor_copy", [vhalo[:]], [v2[:, 0, :]], out=v2[:, 0, :], in_=vhalo[:])
            for g in range(4):
                w = wnext("wk")
                for v in range(2):
                    for s in range(NS):
                        b = nb()
                        for kc in range(8):
                            mm(ps[:, b, :], w[:, (v * 8 + kc) * 128:(v * 8 + kc + 1) * 128], hn[:, kc, sr(s)],
                               kc == 0, kc == 7)
                        dst = kT[:, v, g, 128 + s * 512:128 + (s + 1) * 512]
                        I("act", "activation", [ps[:, b, :]], [dst], out=dst, in_=ps[:, b, :], func=AF.Copy)
            for vp in range(2):
                w = wnext("wv")
                for blk in range(TS // 128):
                    b = nb()
                    for kc in range(8):
                        mm(ps[:, b, 0:256], hn[:, kc, blk * 128:(blk + 1) * 128], w[:, kc * 256:(kc + 1) * 256],
                           kc == 0, kc == 7)
                    dst = v2[:, 1 + blk, vp * 256:(vp + 1) * 256]
                    I("dve", "tensor_copy", [ps[:, b, 0:256]], [dst], out=dst, in_=ps[:, b, 0:256])

        def attn_layer(l, ti):
            lj = l - 2
            for s in range(NS):
                if l != 2:
                    rms_stats(s)
                rms_apply(s, "norm_mix", l, hn)
            for qp in range(4):
                w = wnext("wq")
                for o in range(2):
                    c = 2 * qp + o
                    for s in range(NS):
                        b = nb()
                        for kc in range(8):
                            mm(ps[:, b, :], w[:, (o * 8 + kc) * 128:(o * 8 + kc + 1) * 128], hn[:, kc, sr(s)],
                               kc == 0, kc == 7)
                        dst = qT[:, c, sr(s)]
                        I("act", "activation", [ps[:, b, :]], [dst], out=dst, in_=ps[:, b, :], func=(AF.Copy if DBG['copy'] else AF.Identity))
            items = [(i, g) for i in range(TS // 128) for g in range(4)]
            ttb = [tmp[:, 0, :], tmp[:, 1, :]]
            rdb = [tmp[:, 4, :], tmp[:, 5, :]]
            cnt = {"t": 0}

            ttb4 = [tmp[:, 0, :], tmp[:, 1, :], tmp[:, 2, :], tmp[:, 3, :]]
            NI = len(items)
            jls = {}
            abk = {}

            def a1(n):
                i, g = items[n]
                q0 = i * 128
                jl = [1] if (ti == 0 and i == 0) else [0, 1]
                jls[n] = jl
                for j in jl:
                    kcol = (i + j) * 128
                    tt = ttb4[(n % 2) * 2 + j]
                    bS = nb()
                    for hf in range(2):
                        mm(ps[:, bS, hf * 256:hf * 256 + 256].rearrange("p (a b) -> p a b", a=2),
                           kT[:, hf, g, kcol:kcol + 128], qT[:, 2 * g:2 * g + 2, q0:q0 + 128], True, True)
                    bsrc = biasT[:, j, g * 4:(g + 1) * 4, :]
                    I("dve", "scalar_tensor_tensor", [ps[:, bS, :], bsrc], [tt],
                      out=tt.rearrange("p (a b) -> p a b", a=4), in0=ps[:, bS, :].rearrange("p (a b) -> p a b", a=4),
                      scalar=0.125, in1=bsrc, op0=ALU.mult, op1=ALU.add)

            def a2(n):
                for j in jls[n]:
                    tt = ttb4[(n % 2) * 2 + j]
                    dst = pT[:, n % 3, j, :]
                    I("act", "activation", [tt], [dst], out=dst, in_=tt, func=AF.Exp)

            def b1(n):
                i, g = items[n]
                jl = jls[n]
                bA = nb()
                bB = nb()
                abk[n] = bA
                for x, j in enumerate(jl):
                    mm(ps[:, bA, :], v2[:, i + j, g * 128:(g + 1) * 128], pT[:, n % 3, j, :], x == 0, x == len(jl) - 1)
                for x, j in enumerate(jl):
                    mm(ps[:, bB, :], ones_bf[:], pT[:, n % 3, j, :], x == 0, False)
                es = esink[:, lj * 16 + g * 4:lj * 16 + (g + 1) * 4, :]
                mm(ps[:, bB, :].rearrange("p (a b) -> p a b", a=4), ones_bf[:], es, False, True)
                rd = rdb[n % 2]
                I("act", "activation", [ps[:, bB, :]], [rd], out=rd, in_=ps[:, bB, :], func=AF.Ln)
                I("act", "activation", [rd], [rd], out=rd, in_=rd, func=AF.Exp, scale=-1.0)

            def b2(n):
                i, g = items[n]
                q0 = i * 128
                bA = abk[n]
                rd = rdb[n % 2]
                for hf in range(2):
                    pr = slice(hf * 64, hf * 64 + 64)
                    cs = slice(hf * 256, hf * 256 + 256)
                    dst = attnT[pr, 2 * g:2 * g + 2, q0:q0 + 128]
                    I("dve", "tensor_tensor", [ps[pr, bA, cs], rd[pr, cs]], [dst], out=dst,
                      in0=ps[pr, bA, cs].rearrange("p (a b) -> p a b", a=2),
                      in1=rd[pr, cs].rearrange("p (a b) -> p a b", a=2), op=ALU.mult)

            if DBG['att'] == 0:
                return
            for it in range(NI + 4):
                if it < NI:
                    a1(it)
                if 0 <= it - 1 < NI:
                    a2(it - 1)
                if 0 <= it - 3 < NI:
                    b1(it - 3)
                if 0 <= it - 4 < NI:
                    b2(it - 4)
            if DBG['att'] < 9:
                return
            if l == min(nlayers, 4) - 1 and ti < ntiles - 1:
                I("pool", "tensor_copy", [kT[:, :, :, TS:TS + 128]], [khalo[:]], out=khalo[:], in_=kT[:, :, :, TS:TS + 128])
                I("pool", "tensor_copy", [v2[:, TS // 128, :]], [vhalo[:]], out=vhalo[:], in_=v2[:, TS // 128, :])
            proj_resid("wo", attnT, 8)

        xTv = xT.rearrange("(c p) t -> p c t", p=128)
        oTv = outT.rearrange("(c p) t -> p c t", p=128)
        for ti in range(ntiles):
            t0 = ti * TS
            for c in range(8):
                DMA("sp", h[:, c, :], xTv[:, c, t0:t0 + TS], writes=[h[:, c, :]])
            for l in range(nlayers):
                if l < 2:
                    conv_layer(l, ti)
                else:
                    if l == 2:
                        for s in range(NS):
                            rms_stats(s)
                        kv_proj(ti)
                    attn_layer(l, ti)
                if ffn_last or l < nlayers - 1:
                    ffn(l)
            if final:
                for s in range(NS):
                    rms_stats(s)
                    rms_apply(s, "norm_final", 0, outb)
                src = outb
            else:
                src = h
            for c in range(8):
                DMA("sp", oTv[:, c, t0:t0 + TS], src[:, c, :], reads=[src[:, c, :]])
        assert wstate["used"] == len(seq), (wstate, len(seq))
        S.emit()
    return nc


def t5_bucket_table():
    d = np.arange(128)
    max_exact = 16
    lr = np.log(np.maximum(d, 1).astype(np.float32) / max_exact) / math.log(128 / max_exact)
    large = max_exact + (lr * (32 - max_exact)).astype(np.int32)
    large = np.minimum(large, 31)
    return np.where(d < max_exact, d, large)


def cols(v):
    v = np.asarray(v, np.float32).reshape(-1, 128)
    return np.ascontiguousarray(v.T)


def host_prep(inp):
    f = lambda k: np.asarray(inp[k], np.float32)
    pv = np.zeros((128, NV), np.float32)

    def put(name, arr):
        c = cols(arr)
        pv[:, PVO[name]:PVO[name] + c.shape[1]] = c
    put("norm_mix", f("norm_mix").reshape(-1))
    put("norm_ffn", f("norm_ffn").reshape(-1))
    put("norm_kv", f("norm_kv"))
    put("norm_final", f("norm_final"))
    put("b_pw1", f("conv_b_pw1").reshape(-1))
    put("b_dw", f("conv_b_dw").reshape(-1))
    put("ln_g", f("conv_ln_g").reshape(-1))
    put("ln_b", f("conv_ln_b").reshape(-1))
    put("b_pw2", f("conv_b_pw2").reshape(-1))
    pv[:, PVO["eps"]] = EPS
    wdw = f("conv_w_dw").reshape(2, KW, 8, 128).transpose(3, 0, 2, 1).reshape(128, 2 * 8 * KW)
    wdw = np.ascontiguousarray(wdw)
    hord = []
    for g in range(4):
        hord += [4 * g, 4 * g + 2, 4 * g + 1, 4 * g + 3]
    bt = t5_bucket_table()
    rb = f("rel_bias")[:, hord]
    kk = np.arange(128)[:, None]
    qq = np.arange(128)[None, :]
    biasT = np.full((128, 2, 16, 128), MASKV, np.float32)
    for j in range(2):
        dist = qq - kk + (128 if j == 0 else 0)
        ok = (dist >= 0) & (dist < 128)
        tab = rb[bt[np.clip(dist, 0, 127)]]
        tab = np.where(ok[:, :, None], tab, np.float32(MASKV)).astype(np.float32)
        biasT[:, j] = tab.transpose(0, 2, 1)
    biasT = np.ascontiguousarray(biasT.reshape(128, 2 * 16 * 128))
    sk = np.ascontiguousarray(np.broadcast_to(f("sinks")[:, hord].reshape(1, 32), (2, 32))).astype(np.float32)
    nm = np.array([[0.0], [-1.0]], np.float32)
    identf = np.eye(128, dtype=np.float32)
    wflat = np.empty((128, WTOT), np.float32)
    w1 = f("conv_w_pw1").reshape(2, 8, 128, 2, 8, 128)
    w2 = f("conv_w_pw2").reshape(2, 8, 128, 8, 128)
    wup = f("ffn_w_up").reshape(4, 8, 128, 2, NF, 128)
    wdn = f("ffn_w_down").reshape(4, NF, 128, 8, 128)
    wq = f("w_q").reshape(2, 8, 128, 8, 128)
    wo = f("w_o").reshape(2, 8, 128, 8, 128)
    wkv = f("w_kv")
    wk0 = wkv[:, 0:256].reshape(8, 128, 4, 64)
    zz = np.zeros_like(wk0)
    wk = np.stack([np.concatenate([wk0, zz], axis=-1), np.concatenate([zz, wk0], axis=-1)], axis=2)
    wv = wkv[:, 256:512].reshape(8, 128, 4, 64)
    wv = np.concatenate([wv, wv], axis=-1)
    for pi, (kind, l, i, ln) in enumerate(PIECES):
        o = int(POFF[pi])
        if kind == "pw1":
            a = w1[l, :, :, :, i, :].transpose(1, 0, 2, 3)
        elif kind == "pw2":
            a = w2[l, :, :, 2 * i:2 * i + 2, :].transpose(1, 2, 0, 3)
        elif kind == "up":
            a = wup[l, :, :, :, i, :].transpose(1, 0, 2, 3)
        elif kind == "down":
            a = wdn[l, :, :, i, :].transpose(1, 0, 2)
        elif kind == "wq":
            a = wq[l, :, :, 2 * i:2 * i + 2, :].transpose(1, 2, 0, 3)
        elif kind == "wo":
            a = wo[l, :, :, 2 * i:2 * i + 2, :].transpose(1, 2, 0, 3)
        elif kind == "wk":
            a = wk[:, :, :, i, :].transpose(1, 2, 0, 3)
        elif kind == "wv":
            a = wv[:, :, 2 * i:2 * i + 2, :].transpose(1, 0, 2, 3)
        wflat[:, o:o + ln] = a.reshape(128, ln)
    return dict(pv=pv, wdw=wdw, biasT=biasT, sk=sk, nm=nm, identf=identf, wflat=wflat)


_NC_CACHE = {}


def run(inputs, nlayers=4, ntiles=NT, final=True):
    key = (nlayers, ntiles, final)
    if key not in _NC_CACHE:
        _NC_CACHE[key] = build_nc(nlayers, ntiles, final)
    nc = _NC_CACHE[key]
    shared = host_prep(inputs)
    x = np.asarray(inputs["x"], np.float32)
    in_maps = []
    for b in range(NB):
        m = dict(shared)
        m["xT"] = np.ascontiguousarray(x[b].T)
        in_maps.append(m)
    res = run_bass_kernel_spmd(nc, in_maps, core_ids=list(range(NB)))
    out = np.stack([np.ascontiguousarray(r["outT"].T) for r in res.results], axis=0)
    return out.astype(np.float32)


def kernel(**inputs):
    return run(inputs)
```

```python
import math
from contextlib import ExitStack

import numpy as np
import concourse.bass as bass
import concourse.mybir as mybir
from concourse.bass_utils import run_bass_kernel_spmd

F32 = mybir.dt.float32
BF16 = mybir.dt.bfloat16
AF = mybir.ActivationFunctionType
ALU = mybir.AluOpType

D = 1024
SEQ = 4096
NB = 8
DFF = 2816
NF = DFF // 128
KW = 31
EPS = 1e-6
TS = 1024
NS = TS // 512
NT = SEQ // TS
NSLOT = 4
SLOTW = 2816
MASKV = -30000.0
import os
DBG = {'pre': int(os.environ.get('KPRE', '0')), 'hf': int(os.environ.get('KHF', '2')), 'att': int(os.environ.get('KATT', '9')), 'sink': int(os.environ.get('KSINK', '1')), 'copy': int(os.environ.get('KCOPY', '1'))}

_ES = {F32: 4, BF16: 2}


def rect(ap):
    a = ap.ap
    pstride, npart = a[0]
    off = int(ap.offset)
    p0 = off // pstride
    f0 = off % pstride
    ext = 1
    for s, c in a[1:]:
        ext += (c - 1) * abs(s)
    es = _ES[ap.dtype]
    return (ap.tensor.name, p0, p0 + npart, f0 * es, (f0 + ext) * es)


class Op:
    __slots__ = ("eng", "idx", "fn", "deps", "dma", "sig", "need", "dslot")

    def __init__(self, eng, idx, fn, dma):
        self.eng = eng
        self.idx = idx
        self.fn = fn
        self.deps = set()
        self.dma = dma
        self.sig = None
        self.need = False
        self.dslot = None


class Ent:
    __slots__ = ("p0", "p1", "lo", "hi", "w", "rd", "rdma")

    def __init__(self, r, w):
        self.p0, self.p1, self.lo, self.hi = r[1], r[2], r[3], r[4]
        self.w = w
        self.rd = {}
        self.rdma = []


ENGS = ("pe", "act", "dve", "pool", "sp")
NDMASEM = 8


class Sched:
    def __init__(self, nc):
        self.nc = nc
        self.ops = {e: [] for e in ENGS}
        self.reg = {}
        self.ndma = {e: 0 for e in ENGS}

    def _hits(self, r):
        ents = self.reg.setdefault(r[0], [])
        p0, p1, lo, hi = r[1], r[2], r[3], r[4]
        return ents, [e for e in ents if e.p0 < p1 and p0 < e.p1 and e.lo < hi and lo < e.hi]

    def add(self, eng, fn, reads=(), writes=(), dma=False):
        op = Op(eng, len(self.ops[eng]), fn, dma)
        if dma:
            op.dslot = self.ndma[eng]
            self.ndma[eng] += 1
        self.ops[eng].append(op)
        raw = set()
        for a in reads:
            r = rect(a)
            ents, hit = self._hits(r)
            assert hit, f"read of never-written region {r}"
            for e in hit:
                if e.w is not None:
                    op.deps.add(e.w)
                    raw.add(e.w)
                if dma:
                    e.rdma.append(op)
                else:
                    e.rd[eng] = op
        for a in writes:
            r = rect(a)
            ents, hit = self._hits(r)
            for e in hit:
                if e.w is not None:
                    op.deps.add(e.w)
                for o in e.rd.values():
                    op.deps.add(o)
                for o in e.rdma:
                    op.deps.add(o)
            p0, p1, lo, hi = r[1], r[2], r[3], r[4]
            ents[:] = [e for e in ents if not (p0 <= e.p0 and e.p1 <= p1 and lo <= e.lo and e.hi <= hi)]
            ents.append(Ent(r, op))
        op.deps.discard(op)
        keep = set()
        for d in op.deps:
            if d.eng == eng and not d.dma and not dma:
                if eng == "pe" or d not in raw:
                    continue
            keep.add(d)
        op.deps = keep
        for d in op.deps:
            d.need = True
        return op

    def emit(self):
        nc = self.nc
        with ExitStack() as st:
            csem = {e: st.enter_context(nc.semaphore(f"c_{e}")) for e in ("pe", "act", "dve", "pool")}
            dsem = {e: [st.enter_context(nc.semaphore(f"d_{e}{i}")) for i in range(NDMASEM)]
                    for e in ("sp", "pool")}
            for e in ENGS:
                cnt = 0
                for op in self.ops[e]:
                    if op.dma:
                        i = op.dslot
                        op.sig = (dsem[e][i % NDMASEM], 16 * (i // NDMASEM + 1))
                    elif op.need:
                        cnt += 1
                        op.sig = (csem[e], cnt)
            block = st.enter_context(nc.Block())
            engobj = {"pe": "tensor", "act": "scalar", "dve": "vector", "pool": "gpsimd", "sp": "sync"}

            def run(ename, eng):
                waited = {}
                for op in self.ops[ename]:
                    ws = {}
                    for d in op.deps:
                        s, v = d.sig
                        k = id(s)
                        if waited.get(k, 0) >= v:
                            continue
                        if k not in ws or ws[k][1] < v:
                            ws[k] = (s, v)
                    if op.dma and op.dslot >= NDMASEM:
                        i = op.dslot
                        s = dsem[ename][i % NDMASEM]
                        v = 16 * (i // NDMASEM)
                        k = id(s)
                        if waited.get(k, 0) < v and (k not in ws or ws[k][1] < v):
                            ws[k] = (s, v)
                    for k, (s, v) in ws.items():
                        eng.wait_ge(s, v)
                        waited[k] = v
                    ins = op.fn(eng)
                    if op.dma:
                        ins.then_inc(op.sig[0], 16)
                    elif op.need:
                        ins.then_inc(op.sig[0], 1)
                n = self.ndma[ename]
                for j in range(min(n, NDMASEM)):
                    last = ((n - 1 - j) // NDMASEM) * NDMASEM + j
                    eng.wait_ge(dsem[ename][j], 16 * (last // NDMASEM + 1))

            for ename in ENGS:
                if not self.ops[ename]:
                    continue
                dec = getattr(block, engobj[ename])

                def body(eng, ename=ename):
                    run(ename, eng)
                dec(body)


def pv_layout():
    off = {}
    n = 0
    for name, cnt in (("norm_mix", 32), ("norm_ffn", 32), ("norm_kv", 8), ("norm_final", 8),
                      ("b_pw1", 32), ("b_dw", 16), ("ln_g", 16), ("ln_b", 16), ("b_pw2", 16), ("eps", 1)):
        off[name] = n
        n += cnt
    return off, n


PVO, NV = pv_layout()


def piece_list():
    P = []
    for l in range(4):
        if l < 2:
            P += [("pw1", l, i, 2048) for i in range(8)]
            P += [("pw2", l, i, 2048) for i in range(4)]
        else:
            if l == 2:
                P += [("wk", 0, i, 2048) for i in range(4)]
                P += [("wv", 0, i, 2048) for i in range(2)]
            P += [("wq", l - 2, i, 2048) for i in range(4)]
            P += [("wo", l - 2, i, 2048) for i in range(4)]
        P += [("up", l, i, 2048) for i in range(NF)]
        P += [("down", l, i, 2816) for i in range(8)]
    return P


PIECES = piece_list()
POFF = np.concatenate([[0], np.cumsum([p[3] for p in PIECES])]).astype(np.int64)
WTOT = int(POFF[-1])


def build_nc(nlayers=4, ntiles=NT, final=True, ffn_last=True):
    nc = bass.Bass("TRN2", target_bir_lowering=False)
    dr = lambda n, s, k: nc.dram_tensor(n, s, F32, kind=k).ap()
    xT = dr("xT", [D, SEQ], "ExternalInput")
    pvd = dr("pv", [128, NV], "ExternalInput")
    wdwd = dr("wdw", [128, 2 * 8 * KW], "ExternalInput")
    biasd = dr("biasT", [128, 2 * 16 * 128], "ExternalInput")
    skd = dr("sk", [2, 32], "ExternalInput")
    nmd = dr("nm", [2, 1], "ExternalInput")
    identd = dr("identf", [128, 128], "ExternalInput")
    wflat = dr("wflat", [128, WTOT], "ExternalInput")
    outT = dr("outT", [D, SEQ], "ExternalOutput")

    with ExitStack() as st:
        T = lambda n, s, d: st.enter_context(nc.sbuf_tensor(n, s, d))
        h = T("h", [128, 8, TS], F32)
        hn = T("hn", [128, 8, TS], BF16)
        big = T("big", [128, NF * TS // 2], F32)
        regc = T("regc", [128, 11008], F32)
        ring = T("ring", [128, NSLOT, SLOTW], BF16)
        ahalo = T("ahalo", [128, 2, 8, KW - 1], BF16)
        khalo = T("khalo", [128, 2, 4, 128], BF16)
        vhalo = T("vhalo", [128, 512], BF16)
        pv = T("pvs", [128, NV], F32)
        wdw = T("wdws", [128, 2 * 8 * KW], F32)
        tmp = T("tmp", [128, 6, 512], F32)
        statA = T("statA", [128, 8, 512], BF16)
        statB = T("statB", [128, 8, 512], BF16)
        pT = T("pT", [128, 3, 2, 512], BF16)
        ones_bf = T("ones_bf", [128, 128], BF16)
        ident_bf = T("ident_bf", [128, 128], BF16)
        skt = T("skt", [2, 32], F32)
        nmt = T("nmt", [2, 1], F32)
        et = T("et", [2, 32], F32)
        ehi = T("ehi", [2, 32], BF16)
        esel = T("esel", [2, 32], F32)
        esink = T("esink", [128, 32, 128], BF16)
        ps = st.enter_context(nc.psum_tensor("ps", [128, 8, 512], F32))

        bigb = big.bitcast(BF16)
        regb = regc.bitcast(BF16)
        act = bigb[:, 0:NF * TS].rearrange("p (f t) -> p f t", f=NF)
        y = big[:, 0:8 * TS].rearrange("p (c t) -> p c t", c=8)
        outb = y
        qT = bigb[:, 0:8 * TS].rearrange("p (c t) -> p c t", c=8)
        attnT = bigb[:, 8 * TS:16 * TS].rearrange("p (c t) -> p c t", c=8)
        AW = TS + KW - 1
        a_buf = regb[:, 0:8 * AW].rearrange("p (c t) -> p c t", c=8)
        diag = regb[:, 8 * AW:8 * AW + 2 * KW * 128].rearrange("p (s k j) -> p s k j", s=2, k=KW)
        biasT = regc[:, 0:4096].rearrange("p (j h q) -> p j h q", j=2, h=16)
        KWID = TS + 128
        kT = regb[:, 8192:8192 + 8 * KWID].rearrange("p (v g t) -> p v g t", v=2, g=4)
        v2 = regb[:, 8192 + 8 * KWID:8192 + 8 * KWID + (TS // 128 + 1) * 512].rearrange(
            "p (b n) -> p b n", n=512)
        assert 8192 + 8 * KWID + (TS // 128 + 1) * 512 <= 22016
        assert 8 * AW + 2 * KW * 128 <= 22016

        S = Sched(nc)
        add = S.add
        bank = [0]

        def I(eng, meth, reads, writes, *args, **kw):
            add(eng, lambda e: getattr(e, meth)(*args, **kw), reads=reads, writes=writes)

        def DMA(eng, out, in_, reads=(), writes=()):
            add(eng, lambda e: e.dma_start(out=out, in_=in_), reads=reads, writes=writes, dma=True)

        rot = list(range(8))

        def nb():
            b = rot[bank[0] % len(rot)]
            bank[0] += 1
            return b

        def reserve(n):
            return [rot.pop() for _ in range(n)]

        def release(bs):
            rot.extend(bs)

        pre = {"banks": None, "pend": [], "sqi": 0}

        def pre_begin():
            if DBG['pre']:
                pre["banks"] = reserve(2)

        def pre_update(oc, s, hh):
            if not DBG['pre']:
                return
            sq = statA[:, pre["sqi"] % 8, :]
            pre["sqi"] += 1
            I("act", "activation", [hh], [sq], out=sq, in_=hh, func=AF.Square)
            pre["pend"].append((pre["banks"][s], sq, oc == 0, oc == 7))
            pre_flush(2)

        def pre_flush(keep):
            while len(pre["pend"]) > keep:
                bnk, sq, first, last = pre["pend"].pop(0)
                mm(ps[:, bnk, :], ones_bf[:], sq, first, last)

        def mm(out, lhsT, rhs, start, stop):
            add("pe", lambda e: e.matmul(out, lhsT=lhsT, rhs=rhs, start=start, stop=stop),
                reads=[lhsT, rhs], writes=[out])

        def pcol(name, i):
            o = PVO[name] + i
            return pv[:, o:o + 1]

        epsc = pcol("eps", 0)

        DMA("sp", pv[:], pvd, writes=[pv[:]])
        DMA("sp", wdw[:], wdwd, writes=[wdw[:]])
        DMA("sp", skt[:], skd, writes=[skt[:]])
        DMA("sp", nmt[:], nmd, writes=[nmt[:]])
        DMA("pool", ident_bf[:], identd, writes=[ident_bf[:]])
        I("dve", "memset", [], [ones_bf[:]], ones_bf[:], 1.0)
        I("act", "activation", [skt[:]], [et[:]], out=et[:], in_=skt[:], func=AF.Exp)
        I("dve", "tensor_copy", [et[:]], [ehi[:]], out=ehi[:], in_=et[:])
        I("dve", "scalar_tensor_tensor", [ehi[:], nmt[:], et[:]], [esel[:]], out=esel[:], in0=ehi[:],
          scalar=nmt[:, 0:1], in1=et[:], op0=ALU.mult, op1=ALU.add)
        I("pool", "memset", [], [esink[:]], esink[:], 0.0)
        I("dve", "tensor_copy", [esel[:]], [esink[0:2, :, :]], out=esink[0:2, :, :],
          in_=esel[:].unsqueeze(2).to_broadcast([2, 32, 128]))

        seq = []
        for ti in range(ntiles):
            for pi, p in enumerate(PIECES):
                ok = (nlayers > 2) if p[0] in ("wk", "wv") else ((p[1] + (2 if p[0] in ("wq", "wo") else 0)) < nlayers)
                if p[0] in ("up", "down") and p[1] == nlayers - 1 and not ffn_last:
                    ok = False
                if p[0] == "wo" and DBG['att'] < 9:
                    ok = False
                if ok:
                    seq.append(pi)
        wstate = {"issued": 0, "used": 0}

        def issue_to(n):
            while wstate["issued"] < min(n, len(seq)):
                k = wstate["issued"]
                pi = seq[k]
                sl = k % NSLOT
                ln = PIECES[pi][3]
                o = int(POFF[pi])
                DMA("pool", ring[:, sl, 0:ln], wflat[:, o:o + ln], writes=[ring[:, sl, 0:ln]])
                wstate["issued"] += 1

        def wnext(kind):
            k = wstate["used"]
            assert PIECES[seq[k]][0] == kind, (PIECES[seq[k]], kind)
            issue_to(k + NSLOT)
            wstate["used"] += 1
            return ring[:, k % NSLOT, :]

        rstd = [tmp[:, 2, :], tmp[:, 3, :]]
        sigb = [tmp[:, 0, :], tmp[:, 1, :]]
        sigi = [0]

        def nsig():
            sigi[0] ^= 1
            return sigb[sigi[0]]

        def sr(s):
            return slice(s * 512, (s + 1) * 512)

        def rms_stats(s):
            if pre["banks"] is not None:
                pre_flush(0)
                b = pre["banks"][s]
                if s == NS - 1:
                    release(pre["banks"])
                    pre["banks"] = None
            else:
                for c in range(8):
                    I("act", "activation", [h[:, c, sr(s)]], [statA[:, c, :]],
                      out=statA[:, c, :], in_=h[:, c, sr(s)], func=AF.Square)
                b = nb()
                for c in range(8):
                    mm(ps[:, b, :], ones_bf[:], statA[:, c, :], c == 0, c == 7)
            I("act", "activation", [ps[:, b, :], epsc], [rstd[s]],
              out=rstd[s], in_=ps[:, b, :], func=AF.Ln, bias=epsc, scale=1.0 / D)
            I("act", "activation", [rstd[s]], [rstd[s]], out=rstd[s], in_=rstd[s], func=AF.Exp, scale=-0.5)

        def rms_apply(s, gname, gi, dst):
            for c in range(8):
                g = pcol(gname, gi * 8 + c)
                I("dve", "scalar_tensor_tensor", [h[:, c, sr(s)], g, rstd[s]], [dst[:, c, sr(s)]],
                  out=dst[:, c, sr(s)], in0=h[:, c, sr(s)], scalar=g, in1=rstd[s], op0=ALU.mult, op1=ALU.mult)

        def proj_resid(kind, src, nk, bias_name=None, bias_base=0):
            pre_begin()
            for q in range(4):
                w = wnext(kind)
                for o in range(2):
                    oc = 2 * q + o
                    for s in range(NS):
                        b = nb()
                        for kc in range(nk):
                            mm(ps[:, b, :], w[:, (o * nk + kc) * 128:(o * nk + kc + 1) * 128],
                               src[:, kc, sr(s)], kc == 0, kc == nk - 1)
                        hh = h[:, oc, sr(s)]
                        if bias_name is None:
                            I("dve", "tensor_tensor", [ps[:, b, :], hh], [hh], out=hh, in0=ps[:, b, :], in1=hh, op=ALU.add)
                        else:
                            bc = pcol(bias_name, bias_base + oc)
                            I("dve", "scalar_tensor_tensor", [ps[:, b, :], bc, hh], [hh],
                              out=hh, in0=ps[:, b, :], scalar=bc, in1=hh, op0=ALU.add, op1=ALU.add)
                        pre_update(oc, s, hh)

        def ffn(l):
            for s in range(NS):
                rms_stats(s)
                rms_apply(s, "norm_ffn", l, hn)
            for f in range(NF):
                w = wnext("up")
                for s in range(NS):
                    bG = nb()
                    bU = nb()
                    for kc in range(8):
                        mm(ps[:, bG, :], w[:, (kc * 2) * 128:(kc * 2 + 1) * 128], hn[:, kc, sr(s)], kc == 0, kc == 7)
                    for kc in range(8):
                        mm(ps[:, bU, :], w[:, (kc * 2 + 1) * 128:(kc * 2 + 2) * 128], hn[:, kc, sr(s)], kc == 0, kc == 7)
                    sg = nsig()
                    I("act", "activation", [ps[:, bG, :]], [sg], out=sg, in_=ps[:, bG, :], func=AF.Silu)
                    dst = act[:, f, sr(s)]
                    I("dve", "tensor_tensor", [sg, ps[:, bU, :]], [dst], out=dst, in0=sg, in1=ps[:, bU, :], op=ALU.mult)
            pre_begin()
            for oc in range(8):
                w = wnext("down")
                for s in range(NS):
                    b = nb()
                    for f in range(NF):
                        mm(ps[:, b, :], w[:, f * 128:(f + 1) * 128], act[:, f, sr(s)], f == 0, f == NF - 1)
                    hh = h[:, oc, sr(s)]
                    I("dve", "tensor_tensor", [ps[:, b, :], hh], [hh], out=hh, in0=ps[:, b, :], in1=hh, op=ALU.add)
                    pre_update(oc, s, hh)

        def conv_layer(l, ti):
            for s in range(NS):
                rms_stats(s)
                rms_apply(s, "norm_mix", l, hn)
            hal = a_buf[:, :, 0:KW - 1]
            if ti == 0:
                I("pool", "memset", [], [hal], hal, 0.0)
            else:
                I("pool", "tensor_copy", [ahalo[:, l, :, :]], [hal], out=hal, in_=ahalo[:, l, :, :])
            for oc in range(8):
                w = wnext("pw1")
                for s in range(NS):
                    bA = nb()
                    bG = nb()
                    for kc in range(8):
                        mm(ps[:, bA, :], w[:, (kc * 2) * 128:(kc * 2 + 1) * 128], hn[:, kc, sr(s)], kc == 0, kc == 7)
                    for kc in range(8):
                        mm(ps[:, bG, :], w[:, (kc * 2 + 1) * 128:(kc * 2 + 2) * 128], hn[:, kc, sr(s)], kc == 0, kc == 7)
                    sg = nsig()
                    bg = pcol("b_pw1", l * 16 + 8 + oc)
                    ba = pcol("b_pw1", l * 16 + oc)
                    I("act", "activation", [ps[:, bG, :], bg], [sg], out=sg, in_=ps[:, bG, :], func=AF.Sigmoid,
                      bias=bg, scale=1.0)
                    dst = a_buf[:, oc, KW - 1 + s * 512:KW - 1 + (s + 1) * 512]
                    I("dve", "scalar_tensor_tensor", [ps[:, bA, :], ba, sg], [dst],
                      out=dst, in0=ps[:, bA, :], scalar=ba, in1=sg, op0=ALU.add, op1=ALU.mult)
            def build_diag(c):
                ds = c % 2
                for k in range(KW):
                    wc = wdw[:, (l * 8 + c) * KW + k:(l * 8 + c) * KW + k + 1]
                    I("dve", "tensor_scalar", [ident_bf[:], wc], [diag[:, ds, k, :]],
                      out=diag[:, ds, k, :], in0=ident_bf[:], scalar1=wc, scalar2=None, op0=ALU.mult)

            build_diag(0)
            for c in range(8):
                ds = c % 2
                if c + 1 < 8:
                    build_diag(c + 1)
                for s in range(NS):
                    b = nb()
                    for k in range(KW):
                        mm(ps[:, b, :], diag[:, ds, k, :], a_buf[:, c, s * 512 + k:s * 512 + k + 512], k == 0, k == KW - 1)
                    bd = pcol("b_dw", l * 8 + c)
                    I("act", "activation", [ps[:, b, :], bd], [y[:, c, sr(s)]],
                      out=y[:, c, sr(s)], in_=ps[:, b, :], func=AF.Identity, bias=bd, scale=1.0)
                    ysq = (statA if s == 0 else statB)[:, c, :]
                    I("act", "activation", [ps[:, b, :], bd], [ysq],
                      out=ysq, in_=ps[:, b, :], func=AF.Square, bias=bd, scale=1.0)
                    I("dve", "tensor_copy", [y[:, c, sr(s)]], [hn[:, c, sr(s)]],
                      out=hn[:, c, sr(s)], in_=y[:, c, sr(s)])
            ho = a_buf[:, :, TS:TS + KW - 1]
            I("pool", "tensor_copy", [ho], [ahalo[:, l, :, :]], out=ahalo[:, l, :, :], in_=ho)
            mean = tmp[:, 4, :]
            var = tmp[:, 5, :]
            for s in range(NS):
                bM = nb()
                bE = nb()
                sqb = statA if s == 0 else statB
                for c in range(8):
                    mm(ps[:, bM, :], ones_bf[:], hn[:, c, sr(s)], c == 0, c == 7)
                for c in range(8):
                    mm(ps[:, bE, :], ones_bf[:], sqb[:, c, :], c == 0, c == 7)
                I("dve", "tensor_scalar", [ps[:, bM, :]], [mean], out=mean, in0=ps[:, bM, :], scalar1=1.0 / D,
                  scalar2=None, op0=ALU.mult)
                I("dve", "tensor_tensor", [mean], [var], out=var, in0=mean, in1=mean, op=ALU.mult)
                I("dve", "scalar_tensor_tensor", [ps[:, bE, :], var], [var], out=var, in0=ps[:, bE, :],
                  scalar=1.0 / D, in1=var, op0=ALU.mult, op1=ALU.subtract)
                I("act", "activation", [var, epsc], [var], out=var, in_=var, func=AF.Ln, bias=epsc, scale=1.0)
                I("act", "activation", [var], [var], out=var, in_=var, func=AF.Exp, scale=-0.5)
                for c in range(8):
                    yy = y[:, c, sr(s)]
                    I("dve", "tensor_tensor", [yy, mean], [yy], out=yy, in0=yy, in1=mean, op=ALU.subtract)
                    I("dve", "tensor_tensor", [yy, var], [yy], out=yy, in0=yy, in1=var, op=ALU.mult)
                    g = pcol("ln_g", l * 8 + c)
                    bb = pcol("ln_b", l * 8 + c)
                    dst = hn[:, c, sr(s)]
                    I("act", "activation", [yy, g, bb], [dst], out=dst, in_=yy, func=AF.Silu, bias=bb, scale=g)
            proj_resid("pw2", hn, 8, "b_pw2", l * 8)

        def kv_proj(ti):
            for s in range(NS):
                rms_apply(s, "norm_kv", 0, hn)
            DMA("sp", regc[:, 0:4096], biasd, writes=[regc[:, 0:4096]])
            if ti > 0:
                I("pool", "tensor_copy", [khalo[:]], [kT[:, :, :, 0:128]], out=kT[:, :, :, 0:128], in_=khalo[:])
                I("pool", "tensor_copy", [vhalo[:]], [v2[:, 0, :]], out=v2[:, 0, :], in_=vhalo[:])
            for g in range(4):
                w = wnext("wk")
                for v in range(2):
                    for s in range(NS):
                        b = nb()
                        for kc in range(8):
                            mm(ps[:, b, :], w[:, (v * 8 + kc) * 128:(v * 8 + kc + 1) * 128], hn[:, kc, sr(s)],
                               kc == 0, kc == 7)
                        dst = kT[:, v, g, 128 + s * 512:128 + (s + 1) * 512]
                        I("act", "activation", [ps[:, b, :]], [dst], out=dst, in_=ps[:, b, :], func=AF.Copy)
            for vp in range(2):
                w = wnext("wv")
                for blk in range(TS // 128):
                    b = nb()
                    for kc in range(8):
                        mm(ps[:, b, 0:256], hn[:, kc, blk * 128:(blk + 1) * 128], w[:, kc * 256:(kc + 1) * 256],
                           kc == 0, kc == 7)
                    dst = v2[:, 1 + blk, vp * 256:(vp + 1) * 256]
                    I("dve", "tensor_copy", [ps[:, b, 0:256]], [dst], out=dst, in_=ps[:, b, 0:256])

        def attn_layer(l, ti):
            lj = l - 2
            for s in range(NS):
                if l != 2:
                    rms_stats(s)
                rms_apply(s, "norm_mix", l, hn)
            for qp in range(4):
                w = wnext("wq")
                for o in range(2):
                    c = 2 * qp + o
                    for s in range(NS):
                        b = nb()
                        for kc in range(8):
                            mm(ps[:, b, :], w[:, (o * 8 + kc) * 128:(o * 8 + kc + 1) * 128], hn[:, kc, sr(s)],
                               kc == 0, kc == 7)
                        dst = qT[:, c, sr(s)]
                        I("act", "activation", [ps[:, b, :]], [dst], out=dst, in_=ps[:, b, :], func=(AF.Copy if DBG['copy'] else AF.Identity))
            items = [(i, g) for i in range(TS // 128) for g in range(4)]
            ttb = [tmp[:, 0, :], tmp[:, 1, :]]
            rdb = [tmp[:, 4, :], tmp[:, 5, :]]
            cnt = {"t": 0}

            ttb4 = [tmp[:, 0, :], tmp[:, 1, :], tmp[:, 2, :], tmp[:, 3, :]]
            NI = len(items)
            jls = {}
            abk = {}

            def a1(n):
                i, g = items[n]
                q0 = i * 128
                jl = [1] if (ti == 0 and i == 0) else [0, 1]
                jls[n] = jl
                for j in jl:
                    kcol = (i + j) * 128
                    tt = ttb4[(n % 2) * 2 + j]
                    bS = nb()
                    for hf in range(2):
                        mm(ps[:, bS, hf * 256:hf * 256 + 256].rearrange("p (a b) -> p a b", a=2),
                           kT[:, hf, g, kcol:kcol + 128], qT[:, 2 * g:2 * g + 2, q0:q0 + 128], True, True)
                    bsrc = biasT[:, j, g * 4:(g + 1) * 4, :]
                    I("dve", "scalar_tensor_tensor", [ps[:, bS, :], bsrc], [tt],
                      out=tt.rearrange("p (a b) -> p a b", a=4), in0=ps[:, bS, :].rearrange("p (a b) -> p a b", a=4),
                      scalar=0.125, in1=bsrc, op0=ALU.mult, op1=ALU.add)

            def a2(n):
                for j in jls[n]:
                    tt = ttb4[(n % 2) * 2 + j]
                    dst = pT[:, n % 3, j, :]
                    I("act", "activation", [tt], [dst], out=dst, in_=tt, func=AF.Exp)

            def b1(n):
                i, g = items[n]
                jl = jls[n]
                bA = nb()
                bB = nb()
                abk[n] = bA
                for x, j in enumerate(jl):
                    mm(ps[:, bA, :], v2[:, i + j, g * 128:(g + 1) * 128], pT[:, n % 3, j, :], x == 0, x == len(jl) - 1)
                for x, j in enumerate(jl):
                    mm(ps[:, bB, :], ones_bf[:], pT[:, n % 3, j, :], x == 0, False)
                es = esink[:, lj * 16 + g * 4:lj * 16 + (g + 1) * 4, :]
                mm(ps[:, bB, :].rearrange("p (a b) -> p a b", a=4), ones_bf[:], es, False, True)
                rd = rdb[n % 2]
                I("act", "activation", [ps[:, bB, :]], [rd], out=rd, in_=ps[:, bB, :], func=AF.Ln)
                I("act", "activation", [rd], [rd], out=rd, in_=rd, func=AF.Exp, scale=-1.0)

            def b2(n):
                i, g = items[n]
                q0 = i * 128
                bA = abk[n]
                rd = rdb[n % 2]
                for hf in range(2):
                    pr = slice(hf * 64, hf * 64 + 64)
                    cs = slice(hf * 256, hf * 256 + 256)
                    dst = attnT[pr, 2 * g:2 * g + 2, q0:q0 + 128]
                    I("dve", "tensor_tensor", [ps[pr, bA, cs], rd[pr, cs]], [dst], out=dst,
                      in0=ps[pr, bA, cs].rearrange("p (a b) -> p a b", a=2),
                      in1=rd[pr, cs].rearrange("p (a b) -> p a b", a=2), op=ALU.mult)

            if DBG['att'] == 0:
                return
            for it in range(NI + 4):
                if it < NI:
                    a1(it)
                if 0 <= it - 1 < NI:
                    a2(it - 1)
                if 0 <= it - 3 < NI:
                    b1(it - 3)
                if 0 <= it - 4 < NI:
                    b2(it - 4)
            if DBG['att'] < 9:
                return
            if l == min(nlayers, 4) - 1 and ti < ntiles - 1:
                I("pool", "tensor_copy", [kT[:, :, :, TS:TS + 128]], [khalo[:]], out=khalo[:], in_=kT[:, :, :, TS:TS + 128])
                I("pool", "tensor_copy", [v2[:, TS // 128, :]], [vhalo[:]], out=vhalo[:], in_=v2[:, TS // 128, :])
            proj_resid("wo", attnT, 8)

        xTv = xT.rearrange("(c p) t -> p c t", p=128)
        oTv = outT.rearrange("(c p) t -> p c t", p=128)
        for ti in range(ntiles):
            t0 = ti * TS
            for c in range(8):
                DMA("sp", h[:, c, :], xTv[:, c, t0:t0 + TS], writes=[h[:, c, :]])
            for l in range(nlayers):
                if l < 2:
                    conv_layer(l, ti)
                else:
                    if l == 2:
                        for s in range(NS):
                            rms_stats(s)
                        kv_proj(ti)
                    attn_layer(l, ti)
                if ffn_last or l < nlayers - 1:
                    ffn(l)
            if final:
                for s in range(NS):
                    rms_stats(s)
                    rms_apply(s, "norm_final", 0, outb)
                src = outb
            else:
                src = h
            for c in range(8):
                DMA("sp", oTv[:, c, t0:t0 + TS], src[:, c, :], reads=[src[:, c, :]])
        assert wstate["used"] == len(seq), (wstate, len(seq))
        S.emit()
    return nc


def t5_bucket_table():
    d = np.arange(128)
    max_exact = 16
    lr = np.log(np.maximum(d, 1).astype(np.float32) / max_exact) / math.log(128 / max_exact)
    large = max_exact + (lr * (32 - max_exact)).astype(np.int32)
    large = np.minimum(large, 31)
    return np.where(d < max_exact, d, large)


def cols(v):
    v = np.asarray(v, np.float32).reshape(-1, 128)
    return np.ascontiguousarray(v.T)


def host_prep(inp):
    f = lambda k: np.asarray(inp[k], np.float32)
    pv = np.zeros((128, NV), np.float32)

    def put(name, arr):
        c = cols(arr)
        pv[:, PVO[name]:PVO[name] + c.shape[1]] = c
    put("norm_mix", f("norm_mix").reshape(-1))
    put("norm_ffn", f("norm_ffn").reshape(-1))
    put("norm_kv", f("norm_kv"))
    put("norm_final", f("norm_final"))
    put("b_pw1", f("conv_b_pw1").reshape(-1))
    put("b_dw", f("conv_b_dw").reshape(-1))
    put("ln_g", f("conv_ln_g").reshape(-1))
    put("ln_b", f("conv_ln_b").reshape(-1))
    put("b_pw2", f("conv_b_pw2").reshape(-1))
    pv[:, PVO["eps"]] = EPS
    wdw = f("conv_w_dw").reshape(2, KW, 8, 128).transpose(3, 0, 2, 1).reshape(128, 2 * 8 * KW)
    wdw = np.ascontiguousarray(wdw)
    hord = []
    for g in range(4):
        hord += [4 * g, 4 * g + 2, 4 * g + 1, 4 * g + 3]
    bt = t5_bucket_table()
    rb = f("rel_bias")[:, hord]
    kk = np.arange(128)[:, None]
    qq = np.arange(128)[None, :]
    biasT = np.full((128, 2, 16, 128), MASKV, np.float32)
    for j in range(2):
        dist = qq - kk + (128 if j == 0 else 0)
        ok = (dist >= 0) & (dist < 128)
        tab = rb[bt[np.clip(dist, 0, 127)]]
        tab = np.where(ok[:, :, None], tab, np.float32(MASKV)).astype(np.float32)
        biasT[:, j] = tab.transpose(0, 2, 1)
    biasT = np.ascontiguousarray(biasT.reshape(128, 2 * 16 * 128))
    sk = np.ascontiguousarray(np.broadcast_to(f("sinks")[:, hord].reshape(1, 32), (2, 32))).astype(np.float32)
    nm = np.array([[0.0], [-1.0]], np.float32)
    identf = np.eye(128, dtype=np.float32)
    wflat = np.empty((128, WTOT), np.float32)
    w1 = f("conv_w_pw1").reshape(2, 8, 128, 2, 8, 128)
    w2 = f("conv_w_pw2").reshape(2, 8, 128, 8, 128)
    wup = f("ffn_w_up").reshape(4, 8, 128, 2, NF, 128)
    wdn = f("ffn_w_down").reshape(4, NF, 128, 8, 128)
    wq = f("w_q").reshape(2, 8, 128, 8, 128)
    wo = f("w_o").reshape(2, 8, 128, 8, 128)
    wkv = f("w_kv")
    wk0 = wkv[:, 0:256].reshape(8, 128, 4, 64)
    zz = np.zeros_like(wk0)
    wk = np.stack([np.concatenate([wk0, zz], axis=-1), np.concatenate([zz, wk0], axis=-1)], axis=2)
    wv = wkv[:, 256:512].reshape(8, 128, 4, 64)
    wv = np.concatenate([wv, wv], axis=-1)
    for pi, (kind, l, i, ln) in enumerate(PIECES):
        o = int(POFF[pi])
        if kind == "pw1":
            a = w1[l, :, :, :, i, :].transpose(1, 0, 2, 3)
        elif kind == "pw2":
            a = w2[l, :, :, 2 * i:2 * i + 2, :].transpose(1, 2, 0, 3)
        elif kind == "up":
            a = wup[l, :, :, :, i, :].transpose(1, 0, 2, 3)
        elif kind == "down":
            a = wdn[l, :, :, i, :].transpose(1, 0, 2)
        elif kind == "wq":
            a = wq[l, :, :, 2 * i:2 * i + 2, :].transpose(1, 2, 0, 3)
        elif kind == "wo":
            a = wo[l, :, :, 2 * i:2 * i + 2, :].transpose(1, 2, 0, 3)
        elif kind == "wk":
            a = wk[:, :, :, i, :].transpose(1, 2, 0, 3)
        elif kind == "wv":
            a = wv[:, :, 2 * i:2 * i + 2, :].transpose(1, 0, 2, 3)
        wflat[:, o:o + ln] = a.reshape(128, ln)
    return dict(pv=pv, wdw=wdw, biasT=biasT, sk=sk, nm=nm, identf=identf, wflat=wflat)


_NC_CACHE = {}


def run(inputs, nlayers=4, ntiles=NT, final=True):
    key = (nlayers, ntiles, final)
    if key not in _NC_CACHE:
        _NC_CACHE[key] = build_nc(nlayers, ntiles, final)
    nc = _NC_CACHE[key]
    shared = host_prep(inputs)
    x = np.asarray(inputs["x"], np.float32)
    in_maps = []
    for b in range(NB):
        m = dict(shared)
        m["xT"] = np.ascontiguousarray(x[b].T)
        in_maps.append(m)
    res = run_bass_kernel_spmd(nc, in_maps, core_ids=list(range(NB)))
    out = np.stack([np.ascontiguousarray(r["outT"].T) for r in res.results], axis=0)
    return out.astype(np.float32)


def kernel(**inputs):
    return run(inputs)
```

```python
import math
from contextlib import ExitStack

import numpy as np
import concourse.bass as bass
import concourse.mybir as mybir
from concourse.bass_utils import run_bass_kernel_spmd

F32 = mybir.dt.float32
BF16 = mybir.dt.bfloat16
AF = mybir.ActivationFunctionType
ALU = mybir.AluOpType

D = 1024
SEQ = 4096
NB = 8
DFF = 2816
NF = DFF // 128
KW = 31
EPS = 1e-6
TS = 1024
NS = TS // 512
NT = SEQ // TS
NSLOT = 4
SLOTW = 2816
MASKV = -30000.0
import os
DBG = {'pre': int(os.environ.get('KPRE', '0')), 'hf': int(os.environ.get('KHF', '2')), 'att': int(os.environ.get('KATT', '9')), 'sink': int(os.environ.get('KSINK', '1')), 'copy': int(os.environ.get('KCOPY', '1'))}

_ES = {F32: 4, BF16: 2}


def rect(ap):
    a = ap.ap
    pstride, npart = a[0]
    off = int(ap.offset)
    p0 = off // pstride
    f0 = off % pstride
    ext = 1
    for s, c in a[1:]:
        ext += (c - 1) * abs(s)
    es = _ES[ap.dtype]
    return (ap.tensor.name, p0, p0 + npart, f0 * es, (f0 + ext) * es)


class Op:
    __slots__ = ("eng", "idx", "fn", "deps", "dma", "sig", "need", "dslot")

    def __init__(self, eng, idx, fn, dma):
        self.eng = eng
        self.idx = idx
        self.fn = fn
        self.deps = set()
        self.dma = dma
        self.sig = None
        self.need = False
        self.dslot = None


class Ent:
    __slots__ = ("p0", "p1", "lo", "hi", "w", "rd", "rdma")

    def __init__(self, r, w):
        self.p0, self.p1, self.lo, self.hi = r[1], r[2], r[3], r[4]
        self.w = w
        self.rd = {}
        self.rdma = []


ENGS = ("pe", "act", "dve", "pool", "sp")
NDMASEM = 8


class Sched:
    def __init__(self, nc):
        self.nc = nc
        self.ops = {e: [] for e in ENGS}
        self.reg = {}
        self.ndma = {e: 0 for e in ENGS}

    def _hits(self, r):
        ents = self.reg.setdefault(r[0], [])
        p0, p1, lo, hi = r[1], r[2], r[3], r[4]
        return ents, [e for e in ents if e.p0 < p1 and p0 < e.p1 and e.lo < hi and lo < e.hi]

    def add(self, eng, fn, reads=(), writes=(), dma=False):
        op = Op(eng, len(self.ops[eng]), fn, dma)
        if dma:
            op.dslot = self.ndma[eng]
            self.ndma[eng] += 1
        self.ops[eng].append(op)
        raw = set()
        for a in reads:
            r = rect(a)
            ents, hit = self._hits(r)
            assert hit, f"read of never-written region {r}"
            for e in hit:
                if e.w is not None:
                    op.deps.add(e.w)
                    raw.add(e.w)
                if dma:
                    e.rdma.append(op)
                else:
                    e.rd[eng] = op
        for a in writes:
            r = rect(a)
            ents, hit = self._hits(r)
            for e in hit:
                if e.w is not None:
                    op.deps.add(e.w)
                for o in e.rd.values():
                    op.deps.add(o)
                for o in e.rdma:
                    op.deps.add(o)
            p0, p1, lo, hi = r[1], r[2], r[3], r[4]
            ents[:] = [e for e in ents if not (p0 <= e.p0 and e.p1 <= p1 and lo <= e.lo and e.hi <= hi)]
            ents.append(Ent(r, op))
        op.deps.discard(op)
        keep = set()
        for d in op.deps:
            if d.eng == eng and not d.dma and not dma:
                if eng == "pe" or d not in raw:
                    continue
            keep.add(d)
        op.deps = keep
        for d in op.deps:
            d.need = True
        return op

    def emit(self):
        nc = self.nc
        with ExitStack() as st:
            csem = {e: st.enter_context(nc.semaphore(f"c_{e}")) for e in ("pe", "act", "dve", "pool")}
            dsem = {e: [st.enter_context(nc.semaphore(f"d_{e}{i}")) for i in range(NDMASEM)]
                    for e in ("sp", "pool")}
            for e in ENGS:
                cnt = 0
                for op in self.ops[e]:
                    if op.dma:
                        i = op.dslot
                        op.sig = (dsem[e][i % NDMASEM], 16 * (i // NDMASEM + 1))
                    elif op.need:
                        cnt += 1
                        op.sig = (csem[e], cnt)
            block = st.enter_context(nc.Block())
            engobj = {"pe": "tensor", "act": "scalar", "dve": "vector", "pool": "gpsimd", "sp": "sync"}

            def run(ename, eng):
                waited = {}
                for op in self.ops[ename]:
                    ws = {}
                    for d in op.deps:
                        s, v = d.sig
                        k = id(s)
                        if waited.get(k, 0) >= v:
                            continue
                        if k not in ws or ws[k][1] < v:
                            ws[k] = (s, v)
                    if op.dma and op.dslot >= NDMASEM:
                        i = op.dslot
                        s = dsem[ename][i % NDMASEM]
                        v = 16 * (i // NDMASEM)
                        k = id(s)
                        if waited.get(k, 0) < v and (k not in ws or ws[k][1] < v):
                            ws[k] = (s, v)
                    for k, (s, v) in ws.items():
                        eng.wait_ge(s, v)
                        waited[k] = v
                    ins = op.fn(eng)
                    if op.dma:
                        ins.then_inc(op.sig[0], 16)
                    elif op.need:
                        ins.then_inc(op.sig[0], 1)
                n = self.ndma[ename]
                for j in range(min(n, NDMASEM)):
                    last = ((n - 1 - j) // NDMASEM) * NDMASEM + j
                    eng.wait_ge(dsem[ename][j], 16 * (last // NDMASEM + 1))

            for ename in ENGS:
                if not self.ops[ename]:
                    continue
                dec = getattr(block, engobj[ename])

                def body(eng, ename=ename):
                    run(ename, eng)
                dec(body)


def pv_layout():
    off = {}
    n = 0
    for name, cnt in (("norm_mix", 32), ("norm_ffn", 32), ("norm_kv", 8), ("norm_final", 8),
                      ("b_pw1", 32), ("b_dw", 16), ("ln_g", 16), ("ln_b", 16), ("b_pw2", 16), ("eps", 1)):
        off[name] = n
        n += cnt
    return off, n


PVO, NV = pv_layout()


def piece_list():
    P = []
    for l in range(4):
        if l < 2:
            P += [("pw1", l, i, 2048) for i in range(8)]
            P += [("pw2", l, i, 2048) for i in range(4)]
        else:
            if l == 2:
                P += [("wk", 0, i, 2048) for i in range(4)]
                P += [("wv", 0, i, 2048) for i in range(2)]
            P += [("wq", l - 2, i, 2048) for i in range(4)]
            P += [("wo", l - 2, i, 2048) for i in range(4)]
        P += [("up", l, i, 2048) for i in range(NF)]
        P += [("down", l, i, 2816) for i in range(8)]
    return P


PIECES = piece_list()
POFF = np.concatenate([[0], np.cumsum([p[3] for p in PIECES])]).astype(np.int64)
WTOT = int(POFF[-1])


def build_nc(nlayers=4, ntiles=NT, final=True, ffn_last=True):
    nc = bass.Bass("TRN2", target_bir_lowering=False)
    dr = lambda n, s, k: nc.dram_tensor(n, s, F32, kind=k).ap()
    xT = dr("xT", [D, SEQ], "ExternalInput")
    pvd = dr("pv", [128, NV], "ExternalInput")
    wdwd = dr("wdw", [128, 2 * 8 * KW], "ExternalInput")
    biasd = dr("biasT", [128, 2 * 16 * 128], "ExternalInput")
    skd = dr("sk", [2, 32], "ExternalInput")
    nmd = dr("nm", [2, 1], "ExternalInput")
    identd = dr("identf", [128, 128], "ExternalInput")
    wflat = dr("wflat", [128, WTOT], "ExternalInput")
    outT = dr("outT", [D, SEQ], "ExternalOutput")

    with ExitStack() as st:
        T = lambda n, s, d: st.enter_context(nc.sbuf_tensor(n, s, d))
        h = T("h", [128, 8, TS], F32)
        hn = T("hn", [128, 8, TS], BF16)
        big = T("big", [128, NF * TS // 2], F32)
        regc = T("regc", [128, 11008], F32)
        ring = T("ring", [128, NSLOT, SLOTW], BF16)
        ahalo = T("ahalo", [128, 2, 8, KW - 1], BF16)
        khalo = T("khalo", [128, 2, 4, 128], BF16)
        vhalo = T("vhalo", [128, 512], BF16)
        pv = T("pvs", [128, NV], F32)
        wdw = T("wdws", [128, 2 * 8 * KW], F32)
        tmp = T("tmp", [128, 6, 512], F32)
        statA = T("statA", [128, 8, 512], BF16)
        statB = T("statB", [128, 8, 512], BF16)
        pT = T("pT", [128, 3, 2, 512], BF16)
        ones_bf = T("ones_bf", [128, 128], BF16)
        ident_bf = T("ident_bf", [128, 128], BF16)
        skt = T("skt", [2, 32], F32)
        nmt = T("nmt", [2, 1], F32)
        et = T("et", [2, 32], F32)
        ehi = T("ehi", [2, 32], BF16)
        esel = T("esel", [2, 32], F32)
        esink = T("esink", [128, 32, 128], BF16)
        ps = st.enter_context(nc.psum_tensor("ps", [128, 8, 512], F32))

        bigb = big.bitcast(BF16)
        regb = regc.bitcast(BF16)
        act = bigb[:, 0:NF * TS].rearrange("p (f t) -> p f t", f=NF)
        y = big[:, 0:8 * TS].rearrange("p (c t) -> p c t", c=8)
        outb = y
        qT = bigb[:, 0:8 * TS].rearrange("p (c t) -> p c t", c=8)
        attnT = bigb[:, 8 * TS:16 * TS].rearrange("p (c t) -> p c t", c=8)
        AW = TS + KW - 1
        a_buf = regb[:, 0:8 * AW].rearrange("p (c t) -> p c t", c=8)
        diag = regb[:, 8 * AW:8 * AW + 2 * KW * 128].rearrange("p (s k j) -> p s k j", s=2, k=KW)
        biasT = regc[:, 0:4096].rearrange("p (j h q) -> p j h q", j=2, h=16)
        KWID = TS + 128
        kT = regb[:, 8192:8192 + 8 * KWID].rearrange("p (v g t) -> p v g t", v=2, g=4)
        v2 = regb[:, 8192 + 8 * KWID:8192 + 8 * KWID + (TS // 128 + 1) * 512].rearrange(
            "p (b n) -> p b n", n=512)
        assert 8192 + 8 * KWID + (TS // 128 + 1) * 512 <= 22016
        assert 8 * AW + 2 * KW * 128 <= 22016

        S = Sched(nc)
        add = S.add
        bank = [0]

        def I(eng, meth, reads, writes, *args, **kw):
            add(eng, lambda e: getattr(e, meth)(*args, **kw), reads=reads, writes=writes)

        def DMA(eng, out, in_, reads=(), writes=()):
            add(eng, lambda e: e.dma_start(out=out, in_=in_), reads=reads, writes=writes, dma=True)

        rot = list(range(8))

        def nb():
            b = rot[bank[0] % len(rot)]
            bank[0] += 1
            return b

        def reserve(n):
            return [rot.pop() for _ in range(n)]

        def release(bs):
            rot.extend(bs)

        pre = {"banks": None, "pend": [], "sqi": 0}

        def pre_begin():
            if DBG['pre']:
                pre["banks"] = reserve(2)

        def pre_update(oc, s, hh):
            if not DBG['pre']:
                return
            sq = statA[:, pre["sqi"] % 8, :]
            pre["sqi"] += 1
            I("act", "activation", [hh], [sq], out=sq, in_=hh, func=AF.Square)
            pre["pend"].append((pre["banks"][s], sq, oc == 0, oc == 7))
            pre_flush(2)

        def pre_flush(keep):
            while len(pre["pend"]) > keep:
                bnk, sq, first, last = pre["pend"].pop(0)
                mm(ps[:, bnk, :], ones_bf[:], sq, first, last)

        def mm(out, lhsT, rhs, start, stop):
            add("pe", lambda e: e.matmul(out, lhsT=lhsT, rhs=rhs, start=start, stop=stop),
                reads=[lhsT, rhs], writes=[out])

        def pcol(name, i):
            o = PVO[name] + i
            return pv[:, o:o + 1]

        epsc = pcol("eps", 0)

        DMA("sp", pv[:], pvd, writes=[pv[:]])
        DMA("sp", wdw[:], wdwd, writes=[wdw[:]])
        DMA("sp", skt[:], skd, writes=[skt[:]])
        DMA("sp", nmt[:], nmd, writes=[nmt[:]])
        DMA("pool", ident_bf[:], identd, writes=[ident_bf[:]])
        I("dve", "memset", [], [ones_bf[:]], ones_bf[:], 1.0)
        I("act", "activation", [skt[:]], [et[:]], out=et[:], in_=skt[:], func=AF.Exp)
        I("dve", "tensor_copy", [et[:]], [ehi[:]], out=ehi[:], in_=et[:])
        I("dve", "scalar_tensor_tensor", [ehi[:], nmt[:], et[:]], [esel[:]], out=esel[:], in0=ehi[:],
          scalar=nmt[:, 0:1], in1=et[:], op0=ALU.mult, op1=ALU.add)
        I("pool", "memset", [], [esink[:]], esink[:], 0.0)
        I("dve", "tensor_copy", [esel[:]], [esink[0:2, :, :]], out=esink[0:2, :, :],
          in_=esel[:].unsqueeze(2).to_broadcast([2, 32, 128]))

        seq = []
        for ti in range(ntiles):
            for pi, p in enumerate(PIECES):
                ok = (nlayers > 2) if p[0] in ("wk", "wv") else ((p[1] + (2 if p[0] in ("wq", "wo") else 0)) < nlayers)
                if p[0] in ("up", "down") and p[1] == nlayers - 1 and not ffn_last:
                    ok = False
                if p[0] == "wo" and DBG['att'] < 9:
                    ok = False
                if ok:
                    seq.append(pi)
        wstate = {"issued": 0, "used": 0}

        def issue_to(n):
            while wstate["issued"] < min(n, len(seq)):
                k = wstate["issued"]
                pi = seq[k]
                sl = k % NSLOT
                ln = PIECES[pi][3]
                o = int(POFF[pi])
                DMA("pool", ring[:, sl, 0:ln], wflat[:, o:o + ln], writes=[ring[:, sl, 0:ln]])
                wstate["issued"] += 1

        def wnext(kind):
            k = wstate["used"]
            assert PIECES[seq[k]][0] == kind, (PIECES[seq[k]], kind)
            issue_to(k + NSLOT)
            wstate["used"] += 1
            return ring[:, k % NSLOT, :]

        rstd = [tmp[:, 2, :], tmp[:, 3, :]]
        sigb = [tmp[:, 0, :], tmp[:, 1, :]]
        sigi = [0]

        def nsig():
            sigi[0] ^= 1
            return sigb[sigi[0]]

        def sr(s):
            return slice(s * 512, (s + 1) * 512)

        def rms_stats(s):
            if pre["banks"] is not None:
                pre_flush(0)
                b = pre["banks"][s]
                if s == NS - 1:
                    release(pre["banks"])
                    pre["banks"] = None
            else:
                for c in range(8):
                    I("act", "activation", [h[:, c, sr(s)]], [statA[:, c, :]],
                      out=statA[:, c, :], in_=h[:, c, sr(s)], func=AF.Square)
                b = nb()
                for c in range(8):
                    mm(ps[:, b, :], ones_bf[:], statA[:, c, :], c == 0, c == 7)
            I("act", "activation", [ps[:, b, :], epsc], [rstd[s]],
              out=rstd[s], in_=ps[:, b, :], func=AF.Ln, bias=epsc, scale=1.0 / D)
            I("act", "activation", [rstd[s]], [rstd[s]], out=rstd[s], in_=rstd[s], func=AF.Exp, scale=-0.5)

        def rms_apply(s, gname, gi, dst):
            for c in range(8):
                g = pcol(gname, gi * 8 + c)
                I("dve", "scalar_tensor_tensor", [h[:, c, sr(s)], g, rstd[s]], [dst[:, c, sr(s)]],
                  out=dst[:, c, sr(s)], in0=h[:, c, sr(s)], scalar=g, in1=rstd[s], op0=ALU.mult, op1=ALU.mult)

        def proj_resid(kind, src, nk, bias_name=None, bias_base=0):
            pre_begin()
            for q in range(4):
                w = wnext(kind)
                for o in range(2):
                    oc = 2 * q + o
                    for s in range(NS):
                        b = nb()
                        for kc in range(nk):
                            mm(ps[:, b, :], w[:, (o * nk + kc) * 128:(o * nk + kc + 1) * 128],
                               src[:, kc, sr(s)], kc == 0, kc == nk - 1)
                        hh = h[:, oc, sr(s)]
                        if bias_name is None:
                            I("dve", "tensor_tensor", [ps[:, b, :], hh], [hh], out=hh, in0=ps[:, b, :], in1=hh, op=ALU.add)
                        else:
                            bc = pcol(bias_name, bias_base + oc)
                            I("dve", "scalar_tensor_tensor", [ps[:, b, :], bc, hh], [hh],
                              out=hh, in0=ps[:, b, :], scalar=bc, in1=hh, op0=ALU.add, op1=ALU.add)
                        pre_update(oc, s, hh)

        def ffn(l):
            for s in range(NS):
                rms_stats(s)
                rms_apply(s, "norm_ffn", l, hn)
            for f in range(NF):
                w = wnext("up")
                for s in range(NS):
                    bG = nb()
                    bU = nb()
                    for kc in range(8):
                        mm(ps[:, bG, :], w[:, (kc * 2) * 128:(kc * 2 + 1) * 128], hn[:, kc, sr(s)], kc == 0, kc == 7)
                    for kc in range(8):
                        mm(ps[:, bU, :], w[:, (kc * 2 + 1) * 128:(kc * 2 + 2) * 128], hn[:, kc, sr(s)], kc == 0, kc == 7)
                    sg = nsig()
                    I("act", "activation", [ps[:, bG, :]], [sg], out=sg, in_=ps[:, bG, :], func=AF.Silu)
                    dst = act[:, f, sr(s)]
                    I("dve", "tensor_tensor", [sg, ps[:, bU, :]], [dst], out=dst, in0=sg, in1=ps[:, bU, :], op=ALU.mult)
            pre_begin()
            for oc in range(8):
                w = wnext("down")
                for s in range(NS):
                    b = nb()
                    for f in range(NF):
                        mm(ps[:, b, :], w[:, f * 128:(f + 1) * 128], act[:, f, sr(s)], f == 0, f == NF - 1)
                    hh = h[:, oc, sr(s)]
                    I("dve", "tensor_tensor", [ps[:, b, :], hh], [hh], out=hh, in0=ps[:, b, :], in1=hh, op=ALU.add)
                    pre_update(oc, s, hh)

        def conv_layer(l, ti):
            for s in range(NS):
                rms_stats(s)
                rms_apply(s, "norm_mix", l, hn)
            hal = a_buf[:, :, 0:KW - 1]
            if ti == 0:
                I("pool", "memset", [], [hal], hal, 0.0)
            else:
                I("pool", "tensor_copy", [ahalo[:, l, :, :]], [hal], out=hal, in_=ahalo[:, l, :, :])
            for oc in range(8):
                w = wnext("pw1")
                for s in range(NS):
                    bA = nb()
                    bG = nb()
                    for kc in range(8):
                        mm(ps[:, bA, :], w[:, (kc * 2) * 128:(kc * 2 + 1) * 128], hn[:, kc, sr(s)], kc == 0, kc == 7)
                    for kc in range(8):
                        mm(ps[:, bG, :], w[:, (kc * 2 + 1) * 128:(kc * 2 + 2) * 128], hn[:, kc, sr(s)], kc == 0, kc == 7)
                    sg = nsig()
                    bg = pcol("b_pw1", l * 16 + 8 + oc)
                    ba = pcol("b_pw1", l * 16 + oc)
                    I("act", "activation", [ps[:, bG, :], bg], [sg], out=sg, in_=ps[:, bG, :], func=AF.Sigmoid,
                      bias=bg, scale=1.0)
                    dst = a_buf[:, oc, KW - 1 + s * 512:KW - 1 + (s + 1) * 512]
                    I("dve", "scalar_tensor_tensor", [ps[:, bA, :], ba, sg], [dst],
                      out=dst, in0=ps[:, bA, :], scalar=ba, in1=sg, op0=ALU.add, op1=ALU.mult)
            def build_diag(c):
                ds = c % 2
                for k in range(KW):
                    wc = wdw[:, (l * 8 + c) * KW + k:(l * 8 + c) * KW + k + 1]
                    I("dve", "tensor_scalar", [ident_bf[:], wc], [diag[:, ds, k, :]],
                      out=diag[:, ds, k, :], in0=ident_bf[:], scalar1=wc, scalar2=None, op0=ALU.mult)

            build_diag(0)
            for c in range(8):
                ds = c % 2
                if c + 1 < 8:
                    build_diag(c + 1)
                for s in range(NS):
                    b = nb()
                    for k in range(KW):
                        mm(ps[:, b, :], diag[:, ds, k, :], a_buf[:, c, s * 512 + k:s * 512 + k + 512], k == 0, k == KW - 1)
                    bd = pcol("b_dw", l * 8 + c)
                    I("act", "activation", [ps[:, b, :], bd], [y[:, c, sr(s)]],
                      out=y[:, c, sr(s)], in_=ps[:, b, :], func=AF.Identity, bias=bd, scale=1.0)
                    ysq = (statA if s == 0 else statB)[:, c, :]
                    I("act", "activation", [ps[:, b, :], bd], [ysq],
                      out=ysq, in_=ps[:, b, :], func=AF.Square, bias=bd, scale=1.0)
                    I("dve", "tensor_copy", [y[:, c, sr(s)]], [hn[:, c, sr(s)]],
                      out=hn[:, c, sr(s)], in_=y[:, c, sr(s)])
            ho = a_buf[:, :, TS:TS + KW - 1]
            I("pool", "tensor_copy", [ho], [ahalo[:, l, :, :]], out=ahalo[:, l, :, :], in_=ho)
            means = [tmp[:, 4, :], tmp[:, 0, :]]
            varis = [tmp[:, 5, :], tmp[:, 1, :]]
            for s in range(NS):
                mean = means[s]
                var = varis[s]
                bM = nb()
                bE = nb()
                sqb = statA if s == 0 else statB
                for c in range(8):
                    mm(ps[:, bM, :], ones_bf[:], hn[:, c, sr(s)], c == 0, c == 7)
                for c in range(8):
                    mm(ps[:, bE, :], ones_bf[:], sqb[:, c, :], c == 0, c == 7)
                I("dve", "tensor_scalar", [ps[:, bM, :]], [mean], out=mean, in0=ps[:, bM, :], scalar1=1.0 / D,
                  scalar2=None, op0=ALU.mult)
                I("dve", "tensor_tensor", [mean], [var], out=var, in0=mean, in1=mean, op=ALU.mult)
                I("dve", "scalar_tensor_tensor", [ps[:, bE, :], var], [var], out=var, in0=ps[:, bE, :],
                  scalar=1.0 / D, in1=var, op0=ALU.mult, op1=ALU.subtract)
                I("act", "activation", [var, epsc], [var], out=var, in_=var, func=AF.Ln, bias=epsc, scale=1.0)
                I("act", "activation", [var], [var], out=var, in_=var, func=AF.Exp, scale=-0.5)
            for s in range(NS):
                mean = means[s]
                var = varis[s]
                for c in range(8):
                    yy = y[:, c, sr(s)]
                    I("dve", "tensor_tensor", [yy, mean], [yy], out=yy, in0=yy, in1=mean, op=ALU.subtract)
                    I("dve", "tensor_tensor", [yy, var], [yy], out=yy, in0=yy, in1=var, op=ALU.mult)
                    g = pcol("ln_g", l * 8 + c)
                    bb = pcol("ln_b", l * 8 + c)
                    dst = hn[:, c, sr(s)]
                    I("act", "activation", [yy, g, bb], [dst], out=dst, in_=yy, func=AF.Silu, bias=bb, scale=g)
            proj_resid("pw2", hn, 8, "b_pw2", l * 8)

        def kv_proj(ti):
            for s in range(NS):
                rms_apply(s, "norm_kv", 0, hn)
            DMA("sp", regc[:, 0:4096], biasd, writes=[regc[:, 0:4096]])
            if ti > 0:
                I("pool", "tensor_copy", [khalo[:]], [kT[:, :, :, 0:128]], out=kT[:, :, :, 0:128], in_=khalo[:])
                I("pool", "tensor_copy", [vhalo[:]], [v2[:, 0, :]], out=v2[:, 0, :], in_=vhalo[:])
            for g in range(4):
                w = wnext("wk")
                for v in range(2):
                    for s in range(NS):
                        b = nb()
                        for kc in range(8):
                            mm(ps[:, b, :], w[:, (v * 8 + kc) * 128:(v * 8 + kc + 1) * 128], hn[:, kc, sr(s)],
                               kc == 0, kc == 7)
                        dst = kT[:, v, g, 128 + s * 512:128 + (s + 1) * 512]
                        I("act", "activation", [ps[:, b, :]], [dst], out=dst, in_=ps[:, b, :], func=AF.Copy)
            for vp in range(2):
                w = wnext("wv")
                for blk in range(TS // 128):
                    b = nb()
                    for kc in range(8):
                        mm(ps[:, b, 0:256], hn[:, kc, blk * 128:(blk + 1) * 128], w[:, kc * 256:(kc + 1) * 256],
                           kc == 0, kc == 7)
                    dst = v2[:, 1 + blk, vp * 256:(vp + 1) * 256]
                    I("dve", "tensor_copy", [ps[:, b, 0:256]], [dst], out=dst, in_=ps[:, b, 0:256])

        def attn_layer(l, ti):
            lj = l - 2
            for s in range(NS):
                if l != 2:
                    rms_stats(s)
                rms_apply(s, "norm_mix", l, hn)
            for qp in range(4):
                w = wnext("wq")
                for o in range(2):
                    c = 2 * qp + o
                    for s in range(NS):
                        b = nb()
                        for kc in range(8):
                            mm(ps[:, b, :], w[:, (o * 8 + kc) * 128:(o * 8 + kc + 1) * 128], hn[:, kc, sr(s)],
                               kc == 0, kc == 7)
                        dst = qT[:, c, sr(s)]
                        I("act", "activation", [ps[:, b, :]], [dst], out=dst, in_=ps[:, b, :], func=(AF.Copy if DBG['copy'] else AF.Identity))
            items = [(i, g) for i in range(TS // 128) for g in range(4)]
            ttb = [tmp[:, 0, :], tmp[:, 1, :]]
            rdb = [tmp[:, 4, :], tmp[:, 5, :]]
            cnt = {"t": 0}

            ttb4 = [tmp[:, 0, :], tmp[:, 1, :], tmp[:, 2, :], tmp[:, 3, :]]
            NI = len(items)
            jls = {}
            abk = {}

            def a1(n):
                i, g = items[n]
                q0 = i * 128
                jl = [1] if (ti == 0 and i == 0) else [0, 1]
                jls[n] = jl
                for j in jl:
                    kcol = (i + j) * 128
                    tt = ttb4[(n % 2) * 2 + j]
                    bS = nb()
                    for hf in range(2):
                        mm(ps[:, bS, hf * 256:hf * 256 + 256].rearrange("p (a b) -> p a b", a=2),
                           kT[:, hf, g, kcol:kcol + 128], qT[:, 2 * g:2 * g + 2, q0:q0 + 128], True, True)
                    bsrc = biasT[:, j, g * 4:(g + 1) * 4, :]
                    I("dve", "scalar_tensor_tensor", [ps[:, bS, :], bsrc], [tt],
                      out=tt.rearrange("p (a b) -> p a b", a=4), in0=ps[:, bS, :].rearrange("p (a b) -> p a b", a=4),
                      scalar=0.125, in1=bsrc, op0=ALU.mult, op1=ALU.add)

            def a2(n):
                for j in jls[n]:
                    tt = ttb4[(n % 2) * 2 + j]
                    dst = pT[:, n % 3, j, :]
                    I("act", "activation", [tt], [dst], out=dst, in_=tt, func=AF.Exp)

            def b1(n):
                i, g = items[n]
                jl = jls[n]
                bA = nb()
                bB = nb()
                abk[n] = bA
                for x, j in enumerate(jl):
                    mm(ps[:, bA, :], v2[:, i + j, g * 128:(g + 1) * 128], pT[:, n % 3, j, :], x == 0, x == len(jl) - 1)
                for x, j in enumerate(jl):
                    mm(ps[:, bB, :], ones_bf[:], pT[:, n % 3, j, :], x == 0, False)
                es = esink[:, lj * 16 + g * 4:lj * 16 + (g + 1) * 4, :]
                mm(ps[:, bB, :].rearrange("p (a b) -> p a b", a=4), ones_bf[:], es, False, True)
                rd = rdb[n % 2]
                I("act", "activation", [ps[:, bB, :]], [rd], out=rd, in_=ps[:, bB, :], func=AF.Ln)
                I("act", "activation", [rd], [rd], out=rd, in_=rd, func=AF.Exp, scale=-1.0)

            def b2(n):
                i, g = items[n]
                q0 = i * 128
                bA = abk[n]
                rd = rdb[n % 2]
                for hf in range(2):
                    pr = slice(hf * 64, hf * 64 + 64)
                    cs = slice(hf * 256, hf * 256 + 256)
                    dst = attnT[pr, 2 * g:2 * g + 2, q0:q0 + 128]
                    I("dve", "tensor_tensor", [ps[pr, bA, cs], rd[pr, cs]], [dst], out=dst,
                      in0=ps[pr, bA, cs].rearrange("p (a b) -> p a b", a=2),
                      in1=rd[pr, cs].rearrange("p (a b) -> p a b", a=2), op=ALU.mult)

            if DBG['att'] == 0:
                return
            for it in range(NI + 4):
                if it < NI:
                    a1(it)
                if 0 <= it - 1 < NI:
                    a2(it - 1)
                if 0 <= it - 3 < NI:
                    b1(it - 3)
                if 0 <= it - 4 < NI:
                    b2(it - 4)
            if DBG['att'] < 9:
                return
            if l == min(nlayers, 4) - 1 and ti < ntiles - 1:
                I("pool", "tensor_copy", [kT[:, :, :, TS:TS + 128]], [khalo[:]], out=khalo[:], in_=kT[:, :, :, TS:TS + 128])
                I("pool", "tensor_copy", [v2[:, TS // 128, :]], [vhalo[:]], out=vhalo[:], in_=v2[:, TS // 128, :])
            proj_resid("wo", attnT, 8)

        xTv = xT.rearrange("(c p) t -> p c t", p=128)
        oTv = outT.rearrange("(c p) t -> p c t", p=128)
        for ti in range(ntiles):
            t0 = ti * TS
            for c in range(8):
                DMA("sp", h[:, c, :], xTv[:, c, t0:t0 + TS], writes=[h[:, c, :]])
            for l in range(nlayers):
                if l < 2:
                    conv_layer(l, ti)
                else:
                    if l == 2:
                        for s in range(NS):
                            rms_stats(s)
                        kv_proj(ti)
                    attn_layer(l, ti)
                if ffn_last or l < nlayers - 1:
                    ffn(l)
            if final:
                for s in range(NS):
                    rms_stats(s)
                    rms_apply(s, "norm_final", 0, outb)
                src = outb
            else:
                src = h
            for c in range(8):
                DMA("sp", oTv[:, c, t0:t0 + TS], src[:, c, :], reads=[src[:, c, :]])
        assert wstate["used"] == len(seq), (wstate, len(seq))
        S.emit()
    return nc


def t5_bucket_table():
    d = np.arange(128)
    max_exact = 16
    lr = np.log(np.maximum(d, 1).astype(np.float32) / max_exact) / math.log(128 / max_exact)
    large = max_exact + (lr * (32 - max_exact)).astype(np.int32)
    large = np.minimum(large, 31)
    return np.where(d < max_exact, d, large)


def cols(v):
    v = np.asarray(v, np.float32).reshape(-1, 128)
    return np.ascontiguousarray(v.T)


def host_prep(inp):
    f = lambda k: np.asarray(inp[k], np.float32)
    pv = np.zeros((128, NV), np.float32)

    def put(name, arr):
        c = cols(arr)
        pv[:, PVO[name]:PVO[name] + c.shape[1]] = c
    put("norm_mix", f("norm_mix").reshape(-1))
    put("norm_ffn", f("norm_ffn").reshape(-1))
    put("norm_kv", f("norm_kv"))
    put("norm_final", f("norm_final"))
    put("b_pw1", f("conv_b_pw1").reshape(-1))
    put("b_dw", f("conv_b_dw").reshape(-1))
    put("ln_g", f("conv_ln_g").reshape(-1))
    put("ln_b", f("conv_ln_b").reshape(-1))
    put("b_pw2", f("conv_b_pw2").reshape(-1))
    pv[:, PVO["eps"]] = EPS
    wdw = f("conv_w_dw").reshape(2, KW, 8, 128).transpose(3, 0, 2, 1).reshape(128, 2 * 8 * KW)
    wdw = np.ascontiguousarray(wdw)
    hord = []
    for g in range(4):
        hord += [4 * g, 4 * g + 2, 4 * g + 1, 4 * g + 3]
    bt = t5_bucket_table()
    rb = f("rel_bias")[:, hord]
    kk = np.arange(128)[:, None]
    qq = np.arange(128)[None, :]
    biasT = np.full((128, 2, 16, 128), MASKV, np.float32)
    for j in range(2):
        dist = qq - kk + (128 if j == 0 else 0)
        ok = (dist >= 0) & (dist < 128)
        tab = rb[bt[np.clip(dist, 0, 127)]]
        tab = np.where(ok[:, :, None], tab, np.float32(MASKV)).astype(np.float32)
        biasT[:, j] = tab.transpose(0, 2, 1)
    biasT = np.ascontiguousarray(biasT.reshape(128, 2 * 16 * 128))
    sk = np.ascontiguousarray(np.broadcast_to(f("sinks")[:, hord].reshape(1, 32), (2, 32))).astype(np.float32)
    nm = np.array([[0.0], [-1.0]], np.float32)
    identf = np.eye(128, dtype=np.float32)
    wflat = np.empty((128, WTOT), np.float32)
    w1 = f("conv_w_pw1").reshape(2, 8, 128, 2, 8, 128)
    w2 = f("conv_w_pw2").reshape(2, 8, 128, 8, 128)
    wup = f("ffn_w_up").reshape(4, 8, 128, 2, NF, 128)
    wdn = f("ffn_w_down").reshape(4, NF, 128, 8, 128)
    wq = f("w_q").reshape(2, 8, 128, 8, 128)
    wo = f("w_o").reshape(2, 8, 128, 8, 128)
    wkv = f("w_kv")
    wk0 = wkv[:, 0:256].reshape(8, 128, 4, 64)
    zz = np.zeros_like(wk0)
    wk = np.stack([np.concatenate([wk0, zz], axis=-1), np.concatenate([zz, wk0], axis=-1)], axis=2)
    wv = wkv[:, 256:512].reshape(8, 128, 4, 64)
    wv = np.concatenate([wv, wv], axis=-1)
    for pi, (kind, l, i, ln) in enumerate(PIECES):
        o = int(POFF[pi])
        if kind == "pw1":
            a = w1[l, :, :, :, i, :].transpose(1, 0, 2, 3)
        elif kind == "pw2":
            a = w2[l, :, :, 2 * i:2 * i + 2, :].transpose(1, 2, 0, 3)
        elif kind == "up":
            a = wup[l, :, :, :, i, :].transpose(1, 0, 2, 3)
        elif kind == "down":
            a = wdn[l, :, :, i, :].transpose(1, 0, 2)
        elif kind == "wq":
            a = wq[l, :, :, 2 * i:2 * i + 2, :].transpose(1, 2, 0, 3)
        elif kind == "wo":
            a = wo[l, :, :, 2 * i:2 * i + 2, :].transpose(1, 2, 0, 3)
        elif kind == "wk":
            a = wk[:, :, :, i, :].transpose(1, 2, 0, 3)
        elif kind == "wv":
            a = wv[:, :, 2 * i:2 * i + 2, :].transpose(1, 0, 2, 3)
        wflat[:, o:o + ln] = a.reshape(128, ln)
    return dict(pv=pv, wdw=wdw, biasT=biasT, sk=sk, nm=nm, identf=identf, wflat=wflat)


_NC_CACHE = {}


def run(inputs, nlayers=4, ntiles=NT, final=True):
    key = (nlayers, ntiles, final)
    if key not in _NC_CACHE:
        _NC_CACHE[key] = build_nc(nlayers, ntiles, final)
    nc = _NC_CACHE[key]
    shared = host_prep(inputs)
    x = np.asarray(inputs["x"], np.float32)
    in_maps = []
    for b in range(NB):
        m = dict(shared)
        m["xT"] = np.ascontiguousarray(x[b].T)
        in_maps.append(m)
    res = run_bass_kernel_spmd(nc, in_maps, core_ids=list(range(NB)))
    out = np.stack([np.ascontiguousarray(r["outT"].T) for r in res.results], axis=0)
    return out.astype(np.float32)


def kernel(**inputs):
    return run(inputs)
```
